# Optimizing a Trainium2 kernel written in Bass

```python
import math
import jax
import jax.numpy as jnp
from jax import lax
import numpy as np

D_MODEL = 2048
BATCH = 2
SEQ = 16384
DEPTH = 2

D_MIX = D_MODEL
HEAD_DIM = 64
ATTN_WIDTH = D_MIX // 2
N_Q_HEADS = ATTN_WIDTH // HEAD_DIM
N_KV_HEADS = max(1, N_Q_HEADS // 8)
KV_WIDTH = N_KV_HEADS * HEAD_DIM
WINDOW = 128
ATTN_BLOCK = 128

SSM_WIDTH = D_MIX // 4
SSM_GROUP = 16
SSM_GROUPS = SSM_WIDTH // SSM_GROUP
SSM_STATE = 64
DT_MIN = 1e-3
DT_MAX = 1e-1

GMLP_WIDTH = D_MIX - ATTN_WIDTH - SSM_WIDTH
GMLP_CHUNK = 128
GMLP_HEAD = 128
GMLP_GROUPS = GMLP_WIDTH // GMLP_HEAD

D_IN_PROJ = ATTN_WIDTH + 2 * KV_WIDTH + SSM_WIDTH + 2 * GMLP_WIDTH
D_FF = -(-8 * D_MODEL // (3 * 256)) * 256
EPS = 1e-5

kernel_name = "hybrid_parallel_heads_s5_gmlp_swa"


def rmsnorm(x, g):
    xf = x.astype(jnp.float32)
    y = xf * lax.rsqrt(jnp.mean(xf * xf, axis=-1, keepdims=True) + EPS)
    return (y * g.astype(jnp.float32)).astype(x.dtype)


def layernorm(x, g, b):
    xf = x.astype(jnp.float32)
    xc = xf - jnp.mean(xf, axis=-1, keepdims=True)
    y = xc * lax.rsqrt(jnp.mean(xc * xc, axis=-1, keepdims=True) + EPS)
    return (y * g.astype(jnp.float32) + b.astype(jnp.float32)).astype(x.dtype)


def sliding_window_gqa(q, k, v, sinks):
    bsz, L, _ = q.shape
    nb = L // ATTN_BLOCK
    grp = N_Q_HEADS // N_KV_HEADS
    qb = q.reshape(bsz, nb, ATTN_BLOCK, N_KV_HEADS, grp, HEAD_DIM)

    def band(t):
        tb = t.reshape(bsz, nb, ATTN_BLOCK, N_KV_HEADS, HEAD_DIM)
        prev = jnp.concatenate([jnp.zeros_like(tb[:, :1]), tb[:, :-1]], axis=1)
        return jnp.concatenate([prev, tb], axis=2)

    kb, vb = band(k), band(v)
    scores = jnp.einsum("bnqkgd,bnskd->bnkgqs", qb, kb).astype(jnp.float32) * (HEAD_DIM ** -0.5)
    q_loc = jnp.arange(ATTN_BLOCK)[:, None] + ATTN_BLOCK
    k_loc = jnp.arange(2 * ATTN_BLOCK)[None, :]
    rel = q_loc - k_loc
    in_window = (rel >= 0) & (rel < WINDOW)
    k_abs = jnp.arange(nb)[:, None] * ATTN_BLOCK - ATTN_BLOCK + k_loc
    mask = in_window[None] & (k_abs >= 0)[:, None, :]
    scores = jnp.where(mask[None, :, None, None], scores, -jnp.inf)
    sink = sinks.astype(jnp.float32).reshape(N_KV_HEADS, grp)[None, None, :, :, None, None]
    sink = jnp.broadcast_to(sink, scores.shape[:-1] + (1,))
    probs = jax.nn.softmax(jnp.concatenate([scores, sink], axis=-1), axis=-1)[..., :-1]
    out = jnp.einsum("bnkgqs,bnskd->bnqkgd", probs.astype(v.dtype), vb)
    return out.reshape(bsz, L, ATTN_WIDTH)


def _ssm_combine(c1, c2):
    a1r, a1i, b1r, b1i = c1
    a2r, a2i, b2r, b2i = c2
    return (a2r * a1r - a2i * a1i,
            a2r * a1i + a2i * a1r,
            a2r * b1r - a2i * b1i + b2r,
            a2r * b1i + a2i * b1r + b2i)


def s5_mixer(u, lam_re, lam_im, log_dt, b_re, b_im, c_re, c_im, d_skip, w_glu):
    f32 = jnp.float32
    bsz, L, _ = u.shape
    uf = u.astype(f32)
    ug = uf.reshape(bsz, L, SSM_GROUPS, SSM_GROUP)
    lr, li = lam_re.astype(f32), lam_im.astype(f32)
    dt = jnp.exp(log_dt.astype(f32))[:, None]
    mag = jnp.exp(lr * dt)
    abar_re, abar_im = mag * jnp.cos(li * dt), mag * jnp.sin(li * dt)
    den = lr * lr + li * li
    num_re, num_im = abar_re - 1.0, abar_im
    coef_re = (num_re * lr + num_im * li) / den
    coef_im = (num_im * lr - num_re * li) / den
    br, bi = b_re.astype(f32), b_im.astype(f32)
    bbar_re = coef_re[..., None] * br - coef_im[..., None] * bi
    bbar_im = coef_re[..., None] * bi + coef_im[..., None] * br
    bu_re = jnp.einsum("blgh,gph->lbgp", ug, bbar_re)
    bu_im = jnp.einsum("blgh,gph->lbgp", ug, bbar_im)
    a_re = jnp.broadcast_to(abar_re[None, None], (L, 1, SSM_GROUPS, SSM_STATE))
    a_im = jnp.broadcast_to(abar_im[None, None], (L, 1, SSM_GROUPS, SSM_STATE))
    _, _, s_re, s_im = lax.associative_scan(_ssm_combine, (a_re, a_im, bu_re, bu_im), axis=0)
    y = (jnp.einsum("lbgp,ghp->blgh", s_re, c_re.astype(f32))
         - jnp.einsum("lbgp,ghp->blgh", s_im, c_im.astype(f32)))
    y = y.reshape(bsz, L, SSM_WIDTH) + d_skip.astype(f32) * uf
    y = jax.nn.gelu(y).astype(u.dtype)
    return y * jax.nn.sigmoid(y @ w_glu)


def gmlp_mixer(zu, zv, ln_g, ln_b, w_s, b_s):
    bsz, L, _ = zu.shape
    nc = L // GMLP_CHUNK
    u = jax.nn.gelu(zu)
    v = layernorm(jax.nn.gelu(zv), ln_g, ln_b)
    vc = v.reshape(bsz, nc, GMLP_CHUNK, GMLP_GROUPS, GMLP_HEAD)
    causal = jnp.tril(jnp.ones((GMLP_CHUNK, GMLP_CHUNK), dtype=bool))
    ws = jnp.where(causal[None], w_s, jnp.zeros_like(w_s))
    mixed = jnp.einsum("gts,bcsgh->bctgh", ws, vc) + jnp.swapaxes(b_s, 0, 1)[None, None, :, :, None]
    return u * mixed.reshape(bsz, L, GMLP_WIDTH)


def setup_inputs(seed: int = 0) -> dict:
    key = jax.random.key(seed)
    ks = jax.random.split(key, 26)
    f32 = jnp.float32

    def nrm(k, shape, scale):
        return jax.random.normal(k, shape, f32) * scale

    def gain(k, shape):
        return 1.0 + 0.02 * jax.random.normal(k, shape, f32)

    n_idx = jnp.arange(SSM_STATE, dtype=f32)
    gp = (DEPTH, SSM_GROUPS, SSM_STATE)
    return {
        "x": nrm(ks[0], (BATCH, SEQ, D_MODEL), 1.0),
        "norm_mix": gain(ks[1], (DEPTH, D_MODEL)),
        "w_in": nrm(ks[2], (DEPTH, D_MODEL, D_IN_PROJ), D_MODEL ** -0.5),
        "attn_sinks": nrm(ks[3], (DEPTH, N_Q_HEADS), 0.5),
        "ssm_lam_re": -0.5 + nrm(ks[4], gp, 0.01),
        "ssm_lam_im": math.pi * n_idx + nrm(ks[5], gp, 0.01),
        "ssm_log_dt": jax.random.uniform(ks[6], (DEPTH, SSM_GROUPS), f32,
                                         math.log(DT_MIN), math.log(DT_MAX)),
        "ssm_b_re": nrm(ks[7], (DEPTH, SSM_GROUPS, SSM_STATE, SSM_GROUP), (2 * SSM_GROUP) ** -0.5),
        "ssm_b_im": nrm(ks[8], (DEPTH, SSM_GROUPS, SSM_STATE, SSM_GROUP), (2 * SSM_GROUP) ** -0.5),
        "ssm_c_re": nrm(ks[9], (DEPTH, SSM_GROUPS, SSM_GROUP, SSM_STATE), (2 * SSM_STATE) ** -0.5),
        "ssm_c_im": nrm(ks[10], (DEPTH, SSM_GROUPS, SSM_GROUP, SSM_STATE), (2 * SSM_STATE) ** -0.5),
        "ssm_d": nrm(ks[11], (DEPTH, SSM_WIDTH), 1.0),
        "ssm_w_glu": nrm(ks[12], (DEPTH, SSM_WIDTH, SSM_WIDTH), SSM_WIDTH ** -0.5),
        "gmlp_ln_g": gain(ks[13], (DEPTH, GMLP_WIDTH)),
        "gmlp_ln_b": nrm(ks[14], (DEPTH, GMLP_WIDTH), 0.02),
        "gmlp_w_s": nrm(ks[15], (DEPTH, GMLP_GROUPS, GMLP_CHUNK, GMLP_CHUNK), GMLP_CHUNK ** -0.5),
        "gmlp_b_s": 1.0 + nrm(ks[16], (DEPTH, GMLP_GROUPS, GMLP_CHUNK), 0.1),
        "out_norm_attn": gain(ks[17], (DEPTH, ATTN_WIDTH)),
        "out_norm_ssm": gain(ks[18], (DEPTH, SSM_WIDTH)),
        "out_norm_gmlp": gain(ks[19], (DEPTH, GMLP_WIDTH)),
        "w_out": nrm(ks[20], (DEPTH, D_MIX, D_MODEL), D_MIX ** -0.5),
        "norm_ffn": gain(ks[21], (DEPTH, D_MODEL)),
        "w_gate": nrm(ks[22], (DEPTH, D_MODEL, D_FF), D_MODEL ** -0.5),
        "w_up": nrm(ks[23], (DEPTH, D_MODEL, D_FF), D_MODEL ** -0.5),
        "w_down": nrm(ks[24], (DEPTH, D_FF, D_MODEL), D_FF ** -0.5),
        "norm_final": gain(ks[25], (D_MODEL,)),
    }


def reference(x, norm_mix, w_in, attn_sinks, ssm_lam_re, ssm_lam_im, ssm_log_dt,
              ssm_b_re, ssm_b_im, ssm_c_re, ssm_c_im, ssm_d, ssm_w_glu,
              gmlp_ln_g, gmlp_ln_b, gmlp_w_s, gmlp_b_s,
              out_norm_attn, out_norm_ssm, out_norm_gmlp, w_out,
              norm_ffn, w_gate, w_up, w_down, norm_final):
    splits = [ATTN_WIDTH,
              ATTN_WIDTH + KV_WIDTH,
              ATTN_WIDTH + 2 * KV_WIDTH,
              ATTN_WIDTH + 2 * KV_WIDTH + SSM_WIDTH,
              ATTN_WIDTH + 2 * KV_WIDTH + SSM_WIDTH + GMLP_WIDTH]
    for l in range(DEPTH):
        h = rmsnorm(x, norm_mix[l])
        z = h @ w_in[l]
        q, k, v, u_ssm, z_u, z_v = jnp.split(z, splits, axis=-1)
        y_attn = sliding_window_gqa(q, k, v, attn_sinks[l])
        y_ssm = s5_mixer(u_ssm, ssm_lam_re[l], ssm_lam_im[l], ssm_log_dt[l],
                         ssm_b_re[l], ssm_b_im[l], ssm_c_re[l], ssm_c_im[l],
                         ssm_d[l], ssm_w_glu[l])
        y_gmlp = gmlp_mixer(z_u, z_v, gmlp_ln_g[l], gmlp_ln_b[l], gmlp_w_s[l], gmlp_b_s[l])
        y = jnp.concatenate([rmsnorm(y_attn, out_norm_attn[l]),
                             rmsnorm(y_ssm, out_norm_ssm[l]),
                             rmsnorm(y_gmlp, out_norm_gmlp[l])], axis=-1)
        x = x + y @ w_out[l]
        h = rmsnorm(x, norm_ffn[l])
        x = x + (jax.nn.silu(h @ w_gate[l]) * (h @ w_up[l])) @ w_down[l]
    return rmsnorm(x, norm_final)
```

```python
import math
from contextlib import ExitStack

import numpy as np
import concourse.bass as bass
import concourse.mybir as mybir
from concourse.bass_utils import run_bass_kernel_spmd

F32 = mybir.dt.float32
BF16 = mybir.dt.bfloat16
I32 = mybir.dt.int32
AF = mybir.ActivationFunctionType
ALU = mybir.AluOpType
AX = mybir.AxisListType

NCORES = 8
DBG_STOP = 0
D = 2048
KC = 16
NQH = 16
HD = 64
DIN = 2816
DINP = 3072
EPS = 1e-5
NEG = -30000.0
TWO_PI = 2.0 * math.pi


class Buf:
    def __init__(self, name="", exclusive=False):
        self.name = name
        self.exclusive = exclusive
        self.writers = []
        self.readers = []
        self.deps = []
        self.state = "r"


class Stream:
    def __init__(self, name):
        self.name = name
        self.ops = []
        self.sem = None
        self.count = 0
        self.known = {}


SEM_ROT = 30000


class Prog:
    def __init__(self, nc, n_dma_sems=24):
        self.nc = nc
        self.streams = {n: Stream(n) for n in ["tensor", "vector", "scalar", "gpsimd", "sync"]}
        self.n_dma_sems = n_dma_sems
        self.dma_sems = []
        self.dma_rr = 0
        self.stack = None
        self.nsem = 0
        self.out_tokens = []
        self.coll_tokens = []
        self.stopped = False

    def new_sem(self):
        s = self.stack.enter_context(self.nc.semaphore("s%d" % self.nsem))
        self.nsem += 1
        return s

    def begin(self, stack):
        self.stack = stack
        for st in self.streams.values():
            st.sem = self.new_sem()
        self.dma_sems = [[self.new_sem(), 0] for _ in range(self.n_dma_sems)]
        self.dma_pools = {"sync": self.dma_sems[:16], "gpsimd": self.dma_sems[16:], "scalar": self.dma_sems[16:]}
        self.dma_rr = {"sync": 0, "gpsimd": 0, "scalar": 0}

    def _collect(self, reads, writes, st=None):
        toks = []
        for b in reads:
            toks += b.writers
            if b.exclusive and st is not None:
                toks += [t for t in b.readers if t[0] is not st.sem]
        for b in writes:
            if b.state == "r":
                toks += b.readers + b.writers
            else:
                toks += b.deps
                if b in reads:
                    toks += b.writers
        return toks

    def _commit(self, tok, reads, writes):
        for b in writes:
            if b.state == "r":
                b.deps = b.readers + b.writers
                b.readers = []
                b.writers = [tok]
                b.state = "w"
            else:
                if b in reads:
                    b.deps = b.deps + b.writers
                    b.writers = [tok]
                else:
                    b.writers.append(tok)
        for b in reads:
            if b in writes:
                continue
            b.readers.append(tok)
            b.state = "r"

    def _waits(self, st, toks):
        need = {}
        for sem, val in toks:
            k = id(sem)
            if st.known.get(k, 0) >= val:
                continue
            if k not in need or need[k][1] < val:
                need[k] = (sem, val)
        out = []
        for k, (sem, val) in need.items():
            st.known[k] = val
            out.append((sem, val))
        return out

    def op(self, eng, fn, reads=(), writes=()):
        if self.stopped:
            return None
        st = self.streams[eng]
        reads = list(reads)
        writes = list(writes)
        waits = self._waits(st, self._collect(reads, writes, st))
        if st.count >= SEM_ROT:
            st.sem = self.new_sem()
            st.count = 0
        st.count += 1
        tok = (st.sem, st.count)
        st.ops.append((waits, fn, (st.sem, 1)))
        self._commit(tok, reads, writes)
        return tok

    def fence(self, eng, fn, old, new):
        bufs = []
        for b in list(old) + list(new):
            if b not in bufs:
                bufs.append(b)
        tok = self.op(eng, fn, reads=(), writes=bufs)
        if tok is None:
            return None
        for b in bufs:
            b.readers.append(tok)
            b.state = "r"
        return tok

    def dma(self, queue, out_ap, in_ap, reads=(), writes=(), is_output=False, **kw):
        if self.stopped:
            return None
        st = self.streams[queue]
        reads = list(reads)
        writes = list(writes)
        pool = self.dma_pools[queue]
        ent = pool[self.dma_rr[queue] % len(pool)]
        self.dma_rr[queue] += 1
        toks = self._collect(reads, writes, st)
        if ent[1] > 0:
            toks.append((ent[0], ent[1] * 16))
        if ent[1] >= SEM_ROT // 16:
            ent[0] = self.new_sem()
            ent[1] = 0
        waits = self._waits(st, toks)
        ent[1] += 1
        tok = (ent[0], ent[1] * 16)

        def fn(e, out_ap=out_ap, in_ap=in_ap, kw=kw):
            return e.dma_start(out=out_ap, in_=in_ap, **kw)

        st.ops.append((waits, fn, (ent[0], 16)))
        self._commit(tok, reads, writes)
        if is_output:
            self.out_tokens.append(tok)
        return tok

    def coll(self, fn, reads=(), writes=()):
        st = self.streams["gpsimd"]
        reads = list(reads)
        writes = list(writes)
        waits = self._waits(st, self._collect(reads, writes, st))
        sem = self.new_sem()
        tok = (sem, 1)
        st.ops.append((waits, fn, (sem, 1)))
        self._commit(tok, reads, writes)
        self.coll_tokens.append(tok)
        return tok

    def finish(self):
        st = self.streams["sync"]
        toks = list(self.out_tokens) + list(self.coll_tokens)
        for ent in self.dma_sems:
            if ent[1] > 0:
                toks.append((ent[0], ent[1] * 16))
        for n, s2 in self.streams.items():
            if s2.count > 0:
                toks.append((s2.sem, s2.count))
        waits = self._waits(st, toks)
        st.ops.append((waits, None, None))

    def emit(self, block):
        def replay(st):
            def run(e):
                for waits, fn, inc in st.ops:
                    for sem, val in waits:
                        e.wait_ge(sem, val)
                    if fn is not None:
                        ins = fn(e)
                        ins.then_inc(inc[0], inc[1])
            return run

        block.tensor(replay(self.streams["tensor"]))
        block.vector(replay(self.streams["vector"]))
        block.scalar(replay(self.streams["scalar"]))
        block.gpsimd(replay(self.streams["gpsimd"]))
        block.sync(replay(self.streams["sync"]))


C_ID = 0
C_TRI = 128
C_TV = 256
C_SEL = 385
C_EPS = 409
C_SELH = 410
NPERS = 418
C_MCP = 418
C_MH = 674
C_ONE = 802
NCONST = 930


def host_consts(core, ncores, cps):
    c = np.zeros((128, NCONST), np.float32)
    c[:, C_ID:C_ID + 128] = np.eye(128, dtype=np.float32)
    s = np.arange(128)[:, None]
    t = np.arange(128)[None, :]
    c[:, C_TRI:C_TRI + 128] = (s <= t)
    c[:, C_MCP:C_MCP + 128] = np.where(s <= t, 0.0, NEG)
    c[:, C_MCP + 128:C_MCP + 256] = np.where(s > t, 0.0, NEG)
    c[:, C_TV:C_TV + 129] = np.arange(129, dtype=np.float32)[None, :]
    j = core % cps
    if j == 0:
        c[:, C_MH:C_MH + 128] = NEG
    else:
        c[:, C_MH:C_MH + 128] = np.where(s > t, 0.0, NEG)
    base = core - j
    for i in range(j):
        m = j - 1 - i
        c[:, C_SEL + (base + i) * 3 + m] = 1.0
    if j > 0:
        c[:, C_SELH + core - 1] = 1.0
    c[:, C_ONE:C_ONE + 128] = 1.0
    c[:, C_EPS] = EPS
    return c


class Ctx:
    pass


def build_fused(NTOK, DFF, depth):
    NCH = NTOK // 128
    TT = min(512, NTOK)
    NT = NTOK // TT
    NB = TT // 128
    NHB = DFF // 128
    nc = bass.Bass("TRN2", target_bir_lowering=False)
    K = Ctx()
    K.nc = nc
    dt_in = lambda name, shape, dt=F32: nc.dram_tensor(name, shape, dt, kind="ExternalInput").ap()
    dt_out = lambda name, shape, dt=F32: nc.dram_tensor(name, shape, dt, kind="ExternalOutput").ap()
    dt_int = lambda name, shape, dt=F32: nc.dram_tensor(name, shape, dt, kind="Internal").ap()

    X = dt_in("x", [NTOK, D])
    XH = dt_in("xh", [128, D])
    CN = dt_in("consts", [128, NCONST])
    GFIN = dt_in("gfin", [1, D])
    XO = dt_out("xo", [NTOK, D])
    X1 = dt_int("x1", [NTOK, D])
    XHB = dt_int("xhb", [128, D])
    XHG = dt_int("xhg", [NCORES * 128, D])
    LW, LB, SENDB, SENDG = [], [], [], []
    for l in range(depth):
        sfx = "_%d" % l
        LW.append(dict(
            sp=dt_in("sp" + sfx, [128, 64]), b32=dt_in("b32" + sfx, [128, 2, 16, 32]), gmix=dt_in("gmix" + sfx, [128, KC]),
            w_in=dt_in("w_in" + sfx, [D, DIN]), w_out=dt_in("w_out" + sfx, [D, D]), w_gate=dt_in("w_gate" + sfx, [D, DFF]),
            w_up=dt_in("w_up" + sfx, [D, DFF]), w_down=dt_in("w_down" + sfx, [DFF, D]), w_glu=dt_in("w_glu" + sfx, [512, 512]),
            ctp=dt_in("ctp" + sfx, [128, 2, 16, 128]), ws=dt_in("ws" + sfx, [128, 4, 128]), bs=dt_in("bs" + sfx, [1, 512]),
            lngb=dt_in("lngb" + sfx, [1, 1024]), sinks=dt_in("sinks" + sfx, [1, 16]), gatt=dt_in("gatt" + sfx, [128, 8]),
            gffn=dt_in("gffn" + sfx, [128, KC])))
        LB.append((dt_int("winb" + sfx, [D, DINP], BF16), dt_int("woutb" + sfx, [D, D], BF16), dt_int("wgb" + sfx, [D, DFF], BF16),
                   dt_int("wub" + sfx, [D, DFF], BF16), dt_int("wdb" + sfx, [DFF, D], BF16)))
        SENDB.append(dt_int("sendb" + sfx, [128, 32]))
        SENDG.append(dt_int("sendg" + sfx, [NCORES * 128, 32]))

    with ExitStack() as es:
        P = Prog(nc)
        P.begin(es)
        K.P = P

        K.sbs = {}
        K.bufs = {}

        def sb(name, shape, dt=F32):
            if name not in K.sbs:
                K.sbs[name] = es.enter_context(nc.sbuf_tensor("sb_" + name, shape, dt))
            return K.sbs[name]

        def MB(name, exclusive=False):
            if name not in K.bufs:
                K.bufs[name] = Buf(name, exclusive)
            return K.bufs[name]

        def psum(name, shape, dt=F32):
            return es.enter_context(nc.psum_tensor("ps_" + name, shape, dt))

        NBK = 6
        banks = [psum("bank%d" % i, [128, 512], F32) for i in range(NBK)]
        bankb = [MB("bank%d" % i, exclusive=True) for i in range(NBK)]
        tbanks = [psum("tbank%d" % i, [128, 1024], BF16) for i in range(2)]
        tbankbs = [MB("tbank%d" % i, exclusive=True) for i in range(2)]
        K.rr = 0
        K.trr = 0

        def next_bank():
            i = K.rr
            K.rr = (K.rr + 1) % NBK
            return banks[i], bankb[i]

        def next_tbank():
            i = K.trr
            K.trr = (K.trr + 1) % 2
            return tbanks[i], tbankbs[i]

        cn = sb("cn", [128, NPERS])
        b_cn = MB("cn")
        P.dma("sync", cn[:], CN[:, 0:NPERS], writes=[b_cn])
        cb16 = sb("cb16", [128, 768], BF16)
        b_cb = MB("cb16")
        P.op("vector", lambda e: e.tensor_copy(cb16[:, 0:256], cn[:, 0:256]), reads=[b_cn], writes=[b_cb])
        ident_f = cn[:, C_ID:C_ID + 128]
        tri_f = cn[:, C_TRI:C_TRI + 128]
        tvals = cn[:, C_TV:C_TV + 129]
        eps_c = cn[:, C_EPS:C_EPS + 1]
        ident_b = cb16[:, 0:128]
        tri_b = cb16[:, 128:256]
        mcp_b = cb16[:, 256:512]
        mh_b = cb16[:, 512:640]
        ones_b = cb16[:, 640:768]

        NCHK = NCH
        NTAB = 16 * 129
        b_tmp = [MB("tmp%d" % i) for i in range(5)]
        NBIG = max(NHB, 44) * TT
        big = sb("big", [128, NBIG], BF16)
        bigf = big[:].bitcast(F32)
        TMP = [bigf[:, i * NTAB:(i + 1) * NTAB] for i in range(5)]
        hT = sb("hT", [128, KC, TT], BF16)
        qT = sb("qT", [128, 8, TT], BF16)
        b_wmt, b_bc, b_sloc, b_sin = MB("wmt"), MB("bc"), MB("sloc"), MB("sin")
        xt = sb("xt", [128, NB, D])
        b_xt = [MB("xt%d" % i) for i in range(NB)]
        b_hT = MB("hT")
        EB = [hT[:].rearrange("p k t -> p (k t)")[:, i * 4096:(i + 1) * 4096].rearrange("p (h q) -> p h q", q=256) for i in range(2)]
        b_E = [MB("E0"), MB("E1")]
        wp = sb("wp", [128, 4, 4096], BF16)
        b_wp = [MB("wp%d" % i) for i in range(4)]
        b_qT = MB("qT")
        kT = sb("kT", [128, 2, 2, NB + 1, 128], BF16)
        b_kT = MB("kT")
        P.op("vector", lambda e: e.memset(kT[:], 0.0), reads=[], writes=[b_kT])
        vtok = sb("vtok", [128, NB + 1, 2, 128], BF16)
        b_vtok = MB("vtok")
        actT = big[:, 0:NHB * TT].rearrange("p (h t) -> p h t", t=TT)
        b_act = [MB("act%d" % i) for i in range(NHB)]
        o = [0]

        def carve(n_bf16):
            a = o[0]
            o[0] += n_bf16
            return big[:, a:a + n_bf16]
        ynT = carve(16 * TT).rearrange("p (c t) -> p c t", t=TT)
        b_ynT = MB("ynT")
        uT = carve(4 * TT).rearrange("p (c t) -> p c t", t=TT)
        b_uT = MB("uT")
        guT = carve(4 * TT).rearrange("p (c t) -> p c t", t=TT)
        b_guT = MB("guT")
        vg = carve(4 * TT).rearrange("p (c t) -> p c t", t=512)
        b_vg = MB("vg")
        ypre = carve(2 * 1024).bitcast(F32).rearrange("p (c t) -> p c t", t=128)
        b_ypre = MB("ypre")
        ft0 = carve(2 * 1024).bitcast(F32)
        ft1 = carve(2 * 1024).bitcast(F32)
        b_ft = [MB("ft0"), MB("ft1")]
        ypre_a = carve(2 * 1024).bitcast(F32).rearrange("p (c t) -> p c t", t=128)
        b_ya = MB("ypre_a")
        assert o[0] <= NBIG, o[0]
        zb = sb("zb", [128, 2, 1024], BF16)
        b_zb = MB("zb")
        sbb = sb("sbb", [128, 2, 8, 128], BF16)
        b_sbb = MB("sbb")
        sgt = [zb[:].rearrange("p r c -> p (r c)").bitcast(F32), sbb[:].rearrange("p r g c -> p (r g c)").bitcast(F32)]
        b_sgt = [b_zb, b_sbb]
        st8 = sb("st8", [128, 16])
        b_st8 = MB("st8")
        mixer_bufs = [b_ynT, b_uT, b_guT, b_vg, b_ypre, b_ya] + b_ft
        fsc = sb("fsc", [128, 8])
        b_fsc = MB("fsc")

        def fence_fn(e):
            return e.memset(fsc[:], 0.0)
        xs2 = sb("xs2", [128, 640])
        rr_a = xs2[:, 0:256]
        sq_a = xs2[:, 0:512]
        rs_a = xs2[:, 512:640]
        b_fa = MB("fta")
        xnf = sb("xn", [128, D], BF16)[:].bitcast(F32)
        ypre_g = xnf[:, 0:512].rearrange("p (c t) -> p c t", t=128)
        tmpm_g = xnf[:, 512:640]
        sq_g = xnf[:, 512:768]
        rs_g = xnf[:, 768:896]
        b_yg = b_fg = MB("xn")
        ctr = xt[:, 0, 0:512]
        P.dma("sync", ctr, CN[:, NPERS:NCONST], writes=[b_xt[0]])
        P.op("vector", lambda e: e.tensor_copy(cb16[:, 256:768], ctr), reads=[b_xt[0]], writes=[b_cb])
        ALL_MAIN = [b_hT, b_qT] + mixer_bufs + b_act + b_E
        ALL_SETUP = b_tmp + [b_wmt, b_bc, b_sloc, b_sin]


        def emit_casts(l):
            WIN, WOUT, WG, WU, WD = LW[l]["w_in"], LW[l]["w_out"], LW[l]["w_gate"], LW[l]["w_up"], LW[l]["w_down"]
            WINB, WOUTB, WGB, WUB, WDB = LB[l]
            b_winb, b_woutb, b_wgb, b_wub, b_wdb = [MB(n + str(l)) for n in ("winb", "woutb", "wgb", "wub", "wdb")]
            RB = 256
            for r0 in range(0, D, RB):
                rs = slice(r0, r0 + RB)
                P.dma("gpsimd", WINB[rs, 0:1024], WIN[rs, 0:1024], writes=[b_winb])
                for kv in range(2):
                    for dup in range(2):
                        c0 = 1024 + kv * 128 + dup * 64
                        P.dma("gpsimd", WINB[rs, c0:c0 + 64], WIN[rs, 1024 + kv * 64:1024 + kv * 64 + 64], writes=[b_winb])
                P.dma("gpsimd", WINB[rs, 1280:1408], WIN[rs, 1152:1280], writes=[b_winb])
                P.dma("gpsimd", WINB[rs, 1536:3072], WIN[rs, 1280:2816], writes=[b_winb])
            for r0 in range(0, D, RB):
                rs = slice(r0, r0 + RB)
                P.dma("gpsimd", WOUTB[rs, :], WOUT[rs, :], writes=[b_woutb])
            for r0 in range(0, D, RB):
                rs = slice(r0, r0 + RB)
                P.dma("gpsimd", WGB[rs, :], WG[rs, :], writes=[b_wgb])
                P.dma("gpsimd", WUB[rs, :], WU[rs, :], writes=[b_wub])
            for r0 in range(0, DFF, RB):
                rs = slice(r0, min(DFF, r0 + RB))
                P.dma("gpsimd", WDB[rs, :], WD[rs, :], writes=[b_wdb])


        def emit_layer(l):
            final_norm = (l == depth - 1)
            SP, B32, GMIX = LW[l]["sp"], LW[l]["b32"], LW[l]["gmix"]
            WGLU, CTP, WS, BS, LNGB, SINK, GATT, GFFN = [LW[l][k] for k in ("w_glu", "ctp", "ws", "bs", "lngb", "sinks", "gatt", "gffn")]
            WINB, WOUTB, WGB, WUB, WDB = LB[l]
            b_winb, b_woutb, b_wgb, b_wub, b_wdb = [MB(n + str(l)) for n in ("winb", "woutb", "wgb", "wub", "wdb")]
            WINBv = WINB.rearrange("(kc p) n -> p kc n", p=128)
            Xsrc = X if l == 0 else X1
            b_xsrc = [] if l == 0 else [MB("x1d")]
            Xdst = XO if final_norm else X1
            b_xdst = [] if final_norm else [MB("x1d")]
            P.fence("vector", fence_fn, ALL_MAIN, ALL_SETUP)
            spt = sb("spt", [128, 64])
            b_sp = MB("sp")
            P.dma("sync", spt[:], SP, writes=[b_sp])
            gmix = sb("gmixt", [128, KC])
            b_gmix = MB("gmix")
            P.dma("sync", gmix[:], GMIX, writes=[b_gmix])

            NTAB = 16 * 129
            b_tmp = [MB("tmp%d" % i) for i in range(5)]
            NBIG = max(NHB, 44) * TT
            big = sb("big", [128, NBIG], BF16)
            bigf = big[:].bitcast(F32)
            TMP = [bigf[:, i * NTAB:(i + 1) * NTAB] for i in range(5)]
            hT = sb("hT", [128, KC, TT], BF16)
            hTflat = hT[:].rearrange("p k t -> p (k t)")
            qT = sb("qT", [128, 8, TT], BF16)
            qTflat = qT[:].rearrange("p k t -> p (k t)")
            TI = TMP[2][:].bitcast(I32)

            wm = sb("wm", [128, 2, 16, 128], BF16)
            b_wm = MB("wm")
            wpt = sb("wpt", [128, 2, 16, 129], BF16)
            b_wpt = MB("wpt")
            wmt = hTflat[:, 0:4096].rearrange("p (r g c) -> p r g c", r=2, g=16)
            bc = hTflat[:, 4096:6144].bitcast(F32).rearrange("p (r g c) -> p r g c", r=2, g=16)
            b_wmt = MB("wmt")
            sm = sb("sm", [128, 24, 16])
            b_sm = MB("sm")
            b_bc = MB("bc")

            SMI = dict(dt=0, rho=1, th=2, t0=3, t1=4, thr=5, a_re=6, a_im=7, a127r=8, a127i=9, a128r=10, a128i=11,
                       cr=12, ci=13, den=14, t2=15, t3=16, pnr=17, pni=18, p2r=19, p2i=20, t4=21, t5=22, rden=23)

            def S_(n):
                return sm[:, SMI[n], :]

            lr = spt[:, 0:16]
            li = spt[:, 16:32]
            ldt = spt[:, 32:48]

            def v_(fn, reads, writes):
                return P.op("vector", fn, reads=reads, writes=writes)

            def a_(fn, reads, writes):
                return P.op("scalar", fn, reads=reads, writes=writes)

            a_(lambda e: e.activation(S_("dt"), ldt, AF.Exp), [b_sp], [b_sm])
            v_(lambda e: e.tensor_tensor(S_("rho"), lr, S_("dt"), ALU.mult), [b_sp, b_sm], [b_sm])
            v_(lambda e: e.tensor_tensor(S_("th"), li, S_("dt"), ALU.mult), [b_sp, b_sm], [b_sm])
            smi = sm[:, SMI["t1"], :].bitcast(I32)
            v_(lambda e: e.tensor_scalar(S_("t0"), S_("th"), 1.0 / TWO_PI, None, ALU.mult), [b_sm], [b_sm])
            v_(lambda e: e.tensor_copy(smi, S_("t0")), [b_sm], [b_sm])
            v_(lambda e: e.tensor_copy(S_("t0"), smi), [b_sm], [b_sm])
            v_(lambda e: e.scalar_tensor_tensor(S_("thr"), S_("t0"), -TWO_PI, S_("th"), ALU.mult, ALU.add), [b_sm], [b_sm])

            T3 = lambda i: TMP[i][:].rearrange("p (g t) -> p g t", t=129)
            bc3 = lambda ap: ap.unsqueeze(2).to_broadcast([128, 16, 129])
            tv3 = tvals.unsqueeze(1).to_broadcast([128, 16, 129])

            def reduce_angle(src, dst):
                v_(lambda e: e.tensor_scalar(TMP[1][:], TMP[src][:], 1.0 / TWO_PI, None, ALU.mult), [b_tmp[src]], [b_tmp[1]])
                v_(lambda e: e.tensor_copy(TI, TMP[1][:]), [b_tmp[1]], [b_tmp[2]])
                v_(lambda e: e.tensor_copy(TMP[1][:], TI), [b_tmp[2]], [b_tmp[1]])
                v_(lambda e: e.scalar_tensor_tensor(TMP[dst][:], TMP[1][:], -TWO_PI, TMP[src][:], ALU.mult, ALU.add),
                   [b_tmp[1], b_tmp[src]], [b_tmp[dst]])

            v_(lambda e: e.tensor_tensor(T3(0), bc3(S_("thr")), tv3, ALU.mult), [b_sm, b_cn], [b_tmp[0]])
            reduce_angle(0, 3)
            a_(lambda e: e.activation(TMP[3][:], TMP[3][:], AF.Sin), [b_tmp[3]], [b_tmp[3]])
            v_(lambda e: e.tensor_scalar(TMP[0][:], TMP[0][:], math.pi / 2, None, ALU.add), [b_tmp[0]], [b_tmp[0]])
            reduce_angle(0, 4)
            a_(lambda e: e.activation(TMP[4][:], TMP[4][:], AF.Sin), [b_tmp[4]], [b_tmp[4]])
            v_(lambda e: e.tensor_tensor(T3(0), bc3(S_("rho")), tv3, ALU.mult), [b_sm, b_cn], [b_tmp[0]])
            a_(lambda e: e.activation(TMP[1][:], TMP[0][:], AF.Exp, scale=-1.0), [b_tmp[0]], [b_tmp[1]])
            a_(lambda e: e.activation(TMP[0][:], TMP[0][:], AF.Exp), [b_tmp[0]], [b_tmp[0]])
            v_(lambda e: e.tensor_tensor(wmt[:, 0, :, :], T3(1)[:, :, 0:128], T3(4)[:, :, 0:128], ALU.mult),
               [b_tmp[1], b_tmp[4]], [b_wmt])
            v_(lambda e: e.scalar_tensor_tensor(wmt[:, 1, :, :], T3(1)[:, :, 0:128], -1.0, T3(3)[:, :, 0:128], ALU.mult, ALU.mult),
               [b_tmp[1], b_tmp[3]], [b_wmt])
            v_(lambda e: e.tensor_tensor(TMP[1][:], TMP[0][:], TMP[4][:], ALU.mult), [b_tmp[0], b_tmp[4], b_wmt], [b_tmp[1]])
            v_(lambda e: e.tensor_tensor(TMP[2][:], TMP[0][:], TMP[3][:], ALU.mult), [b_tmp[0], b_tmp[3]], [b_tmp[2]])
            v_(lambda e: e.tensor_copy(wpt[:, 0, :, :], T3(1)), [b_tmp[1]], [b_wpt])
            v_(lambda e: e.tensor_copy(wpt[:, 1, :, :], T3(2)), [b_tmp[2]], [b_wpt])
            for nm, src, col in (("a_re", 1, 1), ("a_im", 2, 1), ("a127r", 1, 127), ("a127i", 2, 127),
                                 ("a128r", 1, 128), ("a128i", 2, 128)):
                v_(lambda e, nm=nm, src=src, col=col: e.tensor_copy(S_(nm), T3(src)[:, :, col]), [b_tmp[src]], [b_sm])
            for ri in range(2):
                for g4 in range(4):
                    tbank, tbankb = next_tbank()

                    def tr(e, ri=ri, g4=g4, tbank=tbank):
                        ins = None
                        for j in range(4):
                            ins = e.transpose(tbank[:, j * 128:(j + 1) * 128], wmt[:, ri, g4 * 4 + j, :], ident_b)
                        return ins
                    P.op("tensor", tr, reads=[b_wmt, b_cb], writes=[tbankb])
                    v_(lambda e, ri=ri, g4=g4, tbank=tbank: e.tensor_copy(wm[:, ri, g4 * 4:(g4 + 1) * 4, :],
                                                                          tbank[:, 0:512].rearrange("p (j c) -> p j c", c=128)),
                       [tbankb], [b_wm])

            def cmul(outr, outi, ar, ai, br, bi, t0="t2", t1="t3"):
                v_(lambda e: e.tensor_tensor(S_(t0), S_(ar), S_(br), ALU.mult), [b_sm], [b_sm])
                v_(lambda e: e.tensor_tensor(S_(t1), S_(ai), S_(bi), ALU.mult), [b_sm], [b_sm])
                v_(lambda e: e.tensor_tensor(S_(outr), S_(t0), S_(t1), ALU.subtract), [b_sm], [b_sm])
                v_(lambda e: e.tensor_tensor(S_(t0), S_(ar), S_(bi), ALU.mult), [b_sm], [b_sm])
                v_(lambda e: e.tensor_tensor(S_(t1), S_(ai), S_(br), ALU.mult), [b_sm], [b_sm])
                v_(lambda e: e.tensor_tensor(S_(outi), S_(t0), S_(t1), ALU.add), [b_sm], [b_sm])

            v_(lambda e: e.tensor_tensor(S_("t0"), lr, lr, ALU.mult), [b_sp], [b_sm])
            v_(lambda e: e.tensor_tensor(S_("t1"), li, li, ALU.mult), [b_sp], [b_sm])
            v_(lambda e: e.tensor_tensor(S_("den"), S_("t0"), S_("t1"), ALU.add), [b_sm], [b_sm])
            v_(lambda e: e.reciprocal(S_("rden"), S_("den")), [b_sm], [b_sm])
            v_(lambda e: e.tensor_scalar(S_("t4"), S_("a_re"), -1.0, None, ALU.add), [b_sm], [b_sm])
            v_(lambda e: e.tensor_tensor(S_("t0"), S_("t4"), lr, ALU.mult), [b_sm, b_sp], [b_sm])
            v_(lambda e: e.tensor_tensor(S_("t1"), S_("a_im"), li, ALU.mult), [b_sm, b_sp], [b_sm])
            v_(lambda e: e.tensor_tensor(S_("t0"), S_("t0"), S_("t1"), ALU.add), [b_sm], [b_sm])
            v_(lambda e: e.tensor_tensor(S_("cr"), S_("t0"), S_("rden"), ALU.mult), [b_sm], [b_sm])
            v_(lambda e: e.tensor_tensor(S_("t0"), S_("a_im"), lr, ALU.mult), [b_sm, b_sp], [b_sm])
            v_(lambda e: e.tensor_tensor(S_("t1"), S_("t4"), li, ALU.mult), [b_sm, b_sp], [b_sm])
            v_(lambda e: e.tensor_tensor(S_("t0"), S_("t0"), S_("t1"), ALU.subtract), [b_sm], [b_sm])
            v_(lambda e: e.tensor_tensor(S_("ci"), S_("t0"), S_("rden"), ALU.mult), [b_sm], [b_sm])
            b32t = TMP[0][:, 0:1024].rearrange("p (r g c) -> p r g c", r=2, g=16)
            P.dma("sync", b32t, B32, reads=[], writes=[b_tmp[0]])
            cb3 = lambda n: S_(n).unsqueeze(2).to_broadcast([128, 16, 32])
            t5 = TMP[1][:, 0:512].rearrange("p (g c) -> p g c", c=32)
            t6 = TMP[1][:, 512:1024].rearrange("p (g c) -> p g c", c=32)
            v_(lambda e: e.tensor_tensor(t5, b32t[:, 0], cb3("cr"), ALU.mult), [b_tmp[0], b_sm], [b_tmp[1]])
            v_(lambda e: e.tensor_tensor(t6, b32t[:, 1], cb3("ci"), ALU.mult), [b_tmp[0], b_sm], [b_tmp[1]])
            v_(lambda e: e.tensor_tensor(bc[:, 0], t5, t6, ALU.subtract), [b_tmp[1]], [b_bc])
            v_(lambda e: e.tensor_tensor(t5, b32t[:, 1], cb3("cr"), ALU.mult), [b_tmp[0], b_sm, b_bc], [b_tmp[1]])
            v_(lambda e: e.tensor_tensor(t6, b32t[:, 0], cb3("ci"), ALU.mult), [b_tmp[0], b_sm], [b_tmp[1]])
            v_(lambda e: e.tensor_tensor(bc[:, 1], t5, t6, ALU.add), [b_tmp[1]], [b_bc])

            if True:
                assert NCH <= 32
                sloc = hTflat[:, 6144:6144 + 2 * 2 * NCH * 16].bitcast(F32).rearrange("p (r c g) -> p r c g", r=2, g=16)
                sin_ = qTflat[:, 0:2 * 2 * (NCH + 1) * 16].bitcast(F32).rearrange("p (r c g) -> p r c g", r=2, g=16)
            b_sloc = MB("sloc")
            b_sin = MB("sin")

            def chain(first_from_zero):
                for c in range(NCH):
                    sr, si = sin_[:, 0, c, :], sin_[:, 1, c, :]
                    v_(lambda e, sr=sr: e.tensor_tensor(S_("t2"), S_("a128r"), sr, ALU.mult), [b_sm, b_sin], [b_sm])
                    v_(lambda e, si=si: e.tensor_tensor(S_("t3"), S_("a128i"), si, ALU.mult), [b_sm, b_sin], [b_sm])
                    v_(lambda e: e.tensor_tensor(S_("t2"), S_("t2"), S_("t3"), ALU.subtract), [b_sm], [b_sm])
                    v_(lambda e, c=c: e.tensor_tensor(sin_[:, 0, c + 1, :], S_("t2"), sloc[:, 0, c, :], ALU.add),
                       [b_sm, b_sloc, b_sin], [b_sin])
                    v_(lambda e, si=si: e.tensor_tensor(S_("t2"), S_("a128r"), si, ALU.mult), [b_sm, b_sin], [b_sm])
                    v_(lambda e, sr=sr: e.tensor_tensor(S_("t3"), S_("a128i"), sr, ALU.mult), [b_sm, b_sin], [b_sm])
                    v_(lambda e: e.tensor_tensor(S_("t2"), S_("t2"), S_("t3"), ALU.add), [b_sm], [b_sm])
                    v_(lambda e, c=c: e.tensor_tensor(sin_[:, 1, c + 1, :], S_("t2"), sloc[:, 1, c, :], ALU.add),
                       [b_sm, b_sloc, b_sin], [b_sin])

            xn = sb("xn", [128, D], BF16)
            b_xn = MB("xn")
            nrm = sb("nrm", [128, 8])
            b_nrm = MB("nrm")
            K.alt = 0

            def norm_block_to_hT(x_ap, b_x, gain_t, b_gain, hT, b_hT, col0, width=D):
                a_(lambda e: e.activation(xn[:], x_ap, AF.Square, accum_out=nrm[:, 0:1]), [b_x], [b_xn, b_nrm])
                a_(lambda e: e.activation(nrm[:, 1:2], nrm[:, 0:1], AF.Sqrt, bias=eps_c, scale=1.0 / width), [b_nrm, b_cn], [b_nrm])
                v_(lambda e: e.reciprocal(nrm[:, 2:3], nrm[:, 1:2]), [b_nrm], [b_nrm])
                a_(lambda e: e.activation(xn[:], x_ap, AF.Copy, scale=nrm[:, 2:3]), [b_x, b_nrm], [b_xn])
                for k4 in range(KC // 4):
                    tbank, tbankb = next_tbank()

                    def tr(e, k4=k4, tbank=tbank):
                        ins = None
                        for j in range(4):
                            kc = k4 * 4 + j
                            ins = e.transpose(tbank[:, j * 128:(j + 1) * 128], xn[:, kc * 128:(kc + 1) * 128], ident_b)
                        return ins
                    P.op("tensor", tr, reads=[b_xn, b_cb], writes=[tbankb])
                    for j in range(4):
                        kc = k4 * 4 + j
                        if (k4 % 2) == 0:
                            a_(lambda e, kc=kc, j=j, tbank=tbank: e.activation(hT[:, kc, col0:col0 + 128], tbank[:, j * 128:(j + 1) * 128],
                                                                               AF.Copy, scale=gain_t[:, kc:kc + 1]),
                               [tbankb, b_gain], [b_hT])
                        else:
                            v_(lambda e, kc=kc, j=j, tbank=tbank: e.tensor_scalar(hT[:, kc, col0:col0 + 128], tbank[:, j * 128:(j + 1) * 128],
                                                                                 gain_t[:, kc:kc + 1], None, ALU.mult),
                               [tbankb, b_gain], [b_hT])

            xa = [bigf[:, 0:2048], bigf[:, 2048:4096]]
            b_xa = [MB("xa%d" % i) for i in range(2)]
            hTa2, utok2 = [], []
            for ti in (2, 4):
                t2b = TMP[ti].bitcast(BF16)
                hTa2.append(t2b[:, 0:2048].rearrange("p (k t) -> p k t", t=128))
                utok2.append(t2b[:, 2048:2560])
            b_hTa2 = [MB("hTa0"), MB("hTa1")]
            b_utok2 = [MB("utok0"), MB("utok1")]
            a_bufs = b_xa + b_hTa2 + b_utok2
            a_tmps = b_tmp[0:3] + [b_tmp[4]]
            wsb = wp[:, 0:2, :].rearrange("p a n -> p (a n)").rearrange("p (k n) -> p k n", n=512)
            P.fence("vector", fence_fn, a_tmps, a_bufs)
            P.dma("sync", wsb, WINBv[:, :, 1536:2048], reads=[b_winb], writes=[b_wp[0], b_wp[1]])
            P.op("vector", lambda e: e.memset(sin_[:, :, 0, :], 0.0), reads=[], writes=[b_sin])

            def a_stage1(c):
                xs, bxs = xa[c % 2], b_xa[c % 2]
                P.dma("sync", xs, Xsrc[c * 128:(c + 1) * 128, :], reads=b_xsrc, writes=[bxs])
                norm_block_to_hT(xs, bxs, gmix, b_gmix, hTa2[c % 2], b_hTa2[c % 2], 0)

            def a_stage2(c):
                hTa, b_hTa = hTa2[c % 2], b_hTa2[c % 2]
                utok, b_utok = utok2[c % 2], b_utok2[c % 2]
                pb, pbb = next_bank()

                def mmu(e, pb=pb):
                    ins = None
                    for kc in range(KC):
                        ins = e.matmul(pb[:], lhsT=hTa[:, kc, :], rhs=wsb[:, kc, :], start=(kc == 0), stop=(kc == KC - 1))
                    return ins
                P.op("tensor", mmu, reads=[b_hTa, b_wp[0], b_wp[1]], writes=[pbb])
                a_(lambda e, pb=pb: e.activation(utok, pb[:], AF.Copy), [pbb], [b_utok])
                pr, prb = next_bank()
                pi, pib = next_bank()

                def mmv(e, pr=pr, pi=pi):
                    ins = None
                    for gp in range(16):
                        e.matmul(pr[:, gp * 32:(gp + 1) * 32], lhsT=wm[:, 0, gp, :], rhs=utok[:, gp * 32:(gp + 1) * 32],
                                 start=True, stop=True)
                        ins = e.matmul(pi[:, gp * 32:(gp + 1) * 32], lhsT=wm[:, 1, gp, :], rhs=utok[:, gp * 32:(gp + 1) * 32],
                                       start=True, stop=True)
                    return ins
                P.op("tensor", mmv, reads=[b_wm, b_utok], writes=[prb, pib])
                f0, f1 = TMP[3][:, 0:512], TMP[3][:, 512:1024]
                bre, bim = bc[:, 0].rearrange("p g c -> p (g c)"), bc[:, 1].rearrange("p g c -> p (g c)")
                v_(lambda e, pr=pr: e.tensor_tensor(f0, bre, pr[:], ALU.mult), [b_bc, prb], [b_tmp[3]])
                v_(lambda e, pi=pi: e.tensor_tensor(f1, bim, pi[:], ALU.mult), [b_bc, pib], [b_tmp[3]])
                v_(lambda e: e.tensor_tensor(f0, f0, f1, ALU.subtract), [b_tmp[3]], [b_tmp[3]])
                v_(lambda e: e.tensor_reduce(S_("pnr"), f0.rearrange("p (g c) -> p g c", c=32), AX.X, ALU.add), [b_tmp[3]], [b_sm])
                v_(lambda e, pi=pi: e.tensor_tensor(f0, bre, pi[:], ALU.mult), [b_bc, pib, b_sm], [b_tmp[3]])
                v_(lambda e, pr=pr: e.tensor_tensor(f1, bim, pr[:], ALU.mult), [b_bc, prb], [b_tmp[3]])
                v_(lambda e: e.tensor_tensor(f0, f0, f1, ALU.add), [b_tmp[3]], [b_tmp[3]])
                v_(lambda e: e.tensor_reduce(S_("pni"), f0.rearrange("p (g c) -> p g c", c=32), AX.X, ALU.add), [b_tmp[3]], [b_sm])
                cmul("p2r", "p2i", "a127r", "a127i", "pnr", "pni")
                v_(lambda e, c=c: e.tensor_copy(sloc[:, 0, c, :], S_("p2r")), [b_sm], [b_sloc])
                v_(lambda e, c=c: e.tensor_copy(sloc[:, 1, c, :], S_("p2i")), [b_sm], [b_sloc])

            a_stage1(0)
            for c in range(NCH):
                if c + 1 < NCH:
                    a_stage1(c + 1)
                a_stage2(c)
            chain(True)
            send_t = sb("send_t", [128, 32])
            b_send = MB("send")
            v_(lambda e: e.tensor_copy(send_t[:, 0:16], sin_[:, 0, NCH, :]), [b_sin], [b_send])
            v_(lambda e: e.tensor_copy(send_t[:, 16:32], sin_[:, 1, NCH, :]), [b_sin], [b_send])
            b_sendb, b_sendg = MB("sendb%d" % l), MB("sendg%d" % l)
            P.dma("sync", SENDB[l], send_t[:], reads=[b_send], writes=[b_sendb])
            P.coll(lambda e: e.collective_compute("AllGather", ALU.bypass, replica_groups=[list(range(NCORES))],
                                                  ins=[SENDB[l].opt()], outs=[SENDG[l].opt()]),
                   reads=[b_sendb], writes=[b_sendg])
            P.fence("vector", fence_fn, a_bufs, a_tmps)
            if l + 1 < depth:
                emit_casts(l + 1)

            ctb = sb("ctb", [128, 2, 16, 128], BF16)
            b_ctb = MB("ctb")
            ctf = TMP[0][:, 0:2048].rearrange("p (g c) -> p g c", c=128)
            for ri in range(2):
                P.dma("sync", ctf, CTP[:, ri], reads=[], writes=[b_tmp[0]])
                if ri == 0:
                    v_(lambda e: e.tensor_copy(ctb[:, 0], ctf), [b_tmp[0]], [b_ctb])
                else:
                    v_(lambda e: e.tensor_scalar(ctb[:, 1], ctf, -1.0, None, ALU.mult), [b_tmp[0]], [b_ctb])
            btb = sb("btb", [128, 2, 16, 128], BF16)
            b_btb = MB("btb")
            bsrc = TMP[3][:].bitcast(BF16)[:, 0:4096].rearrange("p (r g c) -> p r g c", r=2, g=16)
            P.op("vector", lambda e: e.memset(bsrc, 0.0), reads=[], writes=[b_tmp[3]])
            for ri in range(2):
                for j in range(4):
                    src = bc[:, ri].rearrange("p (i j) c -> p i j c", j=4)[:, :, j, :]
                    dst = bsrc[:, ri].rearrange("p (i j) c -> p i j c", j=4)[:, :, j, 32 * j:32 * j + 32]
                    P.op("vector", lambda e, src=src, dst=dst: e.tensor_copy(dst, src), reads=[b_bc], writes=[b_tmp[3]])
            for ri in range(2):
                for g4 in range(4):
                    tbank, tbankb = next_tbank()

                    def tr(e, ri=ri, g4=g4, tbank=tbank):
                        ins = None
                        for j in range(4):
                            ins = e.transpose(tbank[:, j * 128:(j + 1) * 128], bsrc[:, ri, g4 * 4 + j, :], ident_b)
                        return ins
                    P.op("tensor", tr, reads=[b_tmp[3], b_cb], writes=[tbankb])
                    v_(lambda e, ri=ri, g4=g4, tbank=tbank: e.tensor_copy(btb[:, ri, g4 * 4:(g4 + 1) * 4, :],
                                                                          tbank[:, 0:512].rearrange("p (j c) -> p j c", c=128)),
                       [tbankb], [b_btb])
            wglu = sb("wglu", [128, 4, 512], BF16)
            b_wglu = MB("wglu")
            wgf = TMP[4][:, 0:2048].rearrange("p (i c) -> p i c", c=512)
            P.dma("sync", wgf, WGLU.rearrange("(i p) c -> p i c", p=128), reads=[], writes=[b_tmp[4]])
            v_(lambda e: e.tensor_copy(wglu[:], wgf), [b_tmp[4]], [b_wglu])
            wst = sb("wst", [128, 4, 128], BF16)
            b_wst = MB("wst")
            wsf2 = TMP[1][:, 0:512].rearrange("p (g s) -> p g s", s=128)
            P.dma("sync", wsf2, WS, reads=[], writes=[b_tmp[1]])
            for g in range(4):
                pb, pbb = next_bank()
                P.op("tensor", lambda e, pb=pb, g=g: e.transpose(pb[:, 0:128], wsf2[:, g, :], ident_f), reads=[b_tmp[1], b_cn], writes=[pbb])
                v_(lambda e, pb=pb, g=g: e.tensor_tensor(wst[:, g, :], pb[:, 0:128], tri_f, ALU.mult), [pbb, b_cn], [b_wst])
            bsb = sb("bsb", [128, 512])
            lngb = sb("lngb", [128, 1024])
            esink = sb("esink", [128, 16])
            gatt = sb("gatt", [128, 8])
            gffn = sb("gffn", [128, KC])
            b_misc = MB("misc")
            P.dma("sync", bsb[:], BS.partition_broadcast(128), writes=[b_misc])
            P.dma("sync", lngb[:], LNGB.partition_broadcast(128), writes=[b_misc])
            P.dma("sync", esink[:], SINK.partition_broadcast(128), writes=[b_misc])
            P.dma("sync", gatt[:], GATT, writes=[b_misc])
            P.dma("sync", gffn[:], GFFN, writes=[b_misc])
            a_(lambda e: e.activation(esink[:], esink[:], AF.Exp), [b_misc], [b_misc])
            dsk = spt[:, 48:52]
            gssm = spt[:, 52:56]
            ggm = spt[:, 56:60]

            sea = sb("sea", [128, NCORES, 2, 16])
            b_sea = MB("sea")
            P.dma("sync", sea[:].rearrange("p n r g -> p n (r g)"), SENDG[l].rearrange("(n p) c -> p n c", p=128),
                  reads=[MB("sendg%d" % l)], writes=[b_sea])
            v_(lambda e: e.tensor_copy(S_("pnr"), S_("a128r")), [b_sm], [b_sm])
            v_(lambda e: e.tensor_copy(S_("pni"), S_("a128i")), [b_sm], [b_sm])
            nsq = int(round(math.log2(NCH)))
            assert (1 << nsq) == NCH
            for _ in range(nsq):
                cmul("p2r", "p2i", "pnr", "pni", "pnr", "pni")
                v_(lambda e: e.tensor_copy(S_("pnr"), S_("p2r")), [b_sm], [b_sm])
                v_(lambda e: e.tensor_copy(S_("pni"), S_("p2i")), [b_sm], [b_sm])
            cmul("p2r", "p2i", "pnr", "pni", "pnr", "pni")
            tsel = sb("tsel", [128, 3, 2, 16])
            b_tsel = MB("tsel")
            P.op("vector", lambda e: e.memset(tsel[:], 0.0), reads=[], writes=[b_tsel])
            for i in range(NCORES):
                for m in range(3):
                    for ri in range(2):
                        v_(lambda e, i=i, m=m, ri=ri: e.scalar_tensor_tensor(
                            tsel[:, m, ri, :], sea[:, i, ri, :], cn[:, C_SEL + i * 3 + m:C_SEL + i * 3 + m + 1], tsel[:, m, ri, :],
                            ALU.mult, ALU.add), [b_sea, b_cn, b_tsel], [b_tsel])
            for m, (pr_, pi_) in ((1, ("pnr", "pni")), (2, ("p2r", "p2i"))):
                v_(lambda e, m=m, pr_=pr_: e.tensor_tensor(S_("t2"), S_(pr_), tsel[:, m, 0, :], ALU.mult), [b_sm, b_tsel], [b_sm])
                v_(lambda e, m=m, pi_=pi_: e.tensor_tensor(S_("t3"), S_(pi_), tsel[:, m, 1, :], ALU.mult), [b_sm, b_tsel], [b_sm])
                v_(lambda e: e.tensor_tensor(S_("t2"), S_("t2"), S_("t3"), ALU.subtract), [b_sm], [b_sm])
                v_(lambda e: e.tensor_tensor(tsel[:, 0, 0, :], tsel[:, 0, 0, :], S_("t2"), ALU.add), [b_sm, b_tsel], [b_tsel])
                v_(lambda e, m=m, pr_=pr_: e.tensor_tensor(S_("t2"), S_(pr_), tsel[:, m, 1, :], ALU.mult), [b_sm, b_tsel], [b_sm])
                v_(lambda e, m=m, pi_=pi_: e.tensor_tensor(S_("t3"), S_(pi_), tsel[:, m, 0, :], ALU.mult), [b_sm, b_tsel], [b_sm])
                v_(lambda e: e.tensor_tensor(S_("t2"), S_("t2"), S_("t3"), ALU.add), [b_sm], [b_sm])
                v_(lambda e: e.tensor_tensor(tsel[:, 0, 1, :], tsel[:, 0, 1, :], S_("t2"), ALU.add), [b_sm, b_tsel], [b_tsel])
            v_(lambda e: e.tensor_copy(sin_[:, 0, 0, :], tsel[:, 0, 0, :]), [b_tsel], [b_sin])
            v_(lambda e: e.tensor_copy(sin_[:, 1, 0, :], tsel[:, 0, 1, :]), [b_tsel], [b_sin])
            chain(False)
            asin = sb("asin", [128, 2, NCH, 16])
            b_asin = MB("asin")
            abr = lambda n: S_(n).unsqueeze(1).to_broadcast([128, NCH, 16])
            ta = TMP[1][:, 0:NCH * 16].rearrange("p (c g) -> p c g", g=16)
            tb_ = TMP[1][:, 512:512 + NCH * 16].rearrange("p (c g) -> p c g", g=16)
            v_(lambda e: e.tensor_tensor(ta, sin_[:, 0, 0:NCH, :], abr("a_re"), ALU.mult), [b_sin, b_sm, b_wst], [b_tmp[1]])
            v_(lambda e: e.tensor_tensor(tb_, sin_[:, 1, 0:NCH, :], abr("a_im"), ALU.mult), [b_sin, b_sm], [b_tmp[1]])
            v_(lambda e: e.tensor_tensor(asin[:, 0], ta, tb_, ALU.subtract), [b_tmp[1]], [b_asin])
            v_(lambda e: e.tensor_tensor(ta, sin_[:, 0, 0:NCH, :], abr("a_im"), ALU.mult), [b_sin, b_sm, b_asin], [b_tmp[1]])
            v_(lambda e: e.tensor_tensor(tb_, sin_[:, 1, 0:NCH, :], abr("a_re"), ALU.mult), [b_sin, b_sm], [b_tmp[1]])
            v_(lambda e: e.tensor_tensor(asin[:, 1], ta, tb_, ALU.add), [b_tmp[1]], [b_asin])


            K.wjobs = []
            K.wnext = 0
            K.wslot = 0

            def wjob(kind, src_ap, reads):
                K.wjobs.append(dict(kind=kind, src=src_ap, reads=reads, slot=None))
                return len(K.wjobs) - 1

            def wissue_upto(j):
                while K.wnext <= j and K.wnext < len(K.wjobs):
                    jb = K.wjobs[K.wnext]
                    if jb["kind"] == "big":
                        if K.wslot % 2:
                            K.wslot += 1
                        s = K.wslot % 4
                        K.wslot += 2
                        dst = wp[:, s:s + 2, :].rearrange("p a n -> p (a n)")
                        bufs = [b_wp[s], b_wp[s + 1]]
                    else:
                        s = K.wslot % 4
                        K.wslot += 1
                        dst = wp[:, s, :]
                        bufs = [b_wp[s]]
                    src = jb["src"]
                    dshape = src.shape
                    if len(dshape) == 3:
                        dstv = dst[:, 0:dshape[1] * dshape[2]].rearrange("p (a n) -> p a n", n=dshape[2])
                    else:
                        dstv = dst[:, 0:dshape[1]]
                    P.dma("sync", dstv, src, reads=jb["reads"], writes=bufs)
                    jb["slot"] = (dstv, bufs)
                    K.wnext += 1

            def wget(j):
                wissue_upto(j + 1)
                return K.wjobs[j]["slot"]

            WINBv = WINB.rearrange("(kc p) n -> p kc n", p=128)
            WOUTBv = WOUTB.rearrange("(kc p) n -> p kc n", p=128)
            WGBv = WGB.rearrange("(kc p) n -> p kc n", p=128)
            WUBv = WUB.rearrange("(kc p) n -> p kc n", p=128)
            WDBv = WDB.rearrange("(hb p) n -> p hb n", p=128)
            HPS = 2
            NST = NHB // HPS
            DPC = 16
            pieces = [(a, min(NHB, a + DPC)) for a in range(0, NHB, DPC)]
            hj = wjob("big", WINBv[:, :, 1024:1536], [b_winb])
            tile_jobs = []
            for T in range(NT):
                J = {}
                J["in"] = [wjob("big", WINBv[:, :, i * 512:(i + 1) * 512], [b_winb]) for i in range(6)]
                J["out"] = [wjob("big", WOUTBv[:, :, i * 512:(i + 1) * 512], [b_woutb]) for i in range(4)]
                J["gu"] = []
                for s_ in range(NST):
                    J["gu"].append((wjob("half", WGBv[:, :, s_ * 256:(s_ + 1) * 256], [b_wgb]),
                                    wjob("half", WUBv[:, :, s_ * 256:(s_ + 1) * 256], [b_wub])))
                J["dn"] = [[wjob("big", WDBv[:, a:b, cb * 512:(cb + 1) * 512], [b_wdb]) for (a, b) in pieces] for cb in range(4)]
                tile_jobs.append(J)

            def evac(i, fn_act, fn_vec, reads, writes):
                if i % 2 == 0:
                    a_(fn_act, reads, writes)
                else:
                    v_(fn_vec, reads, writes)

            def rms_feature_major(yp, b_yp, sqf, rs, b_scr, nchunk, width, gain_ap, dst_chunk0, tok0):
                sq = sqf[:, 0:nchunk * 64].bitcast(BF16).rearrange("p (c t) -> p c t", t=128)
                a_(lambda e: e.activation(sq, yp[:, 0:nchunk, :], AF.Square), [b_yp], b_scr)
                pb, pbb = next_bank()

                def mmss(e, pb=pb):
                    ins = None
                    for c in range(nchunk):
                        ins = e.matmul(pb[:, 0:128], lhsT=ones_b, rhs=sq[:, c, :], start=(c == 0), stop=(c == nchunk - 1))
                    return ins
                P.op("tensor", mmss, reads=b_scr + [b_cb], writes=[pbb])
                a_(lambda e, pb=pb: e.activation(rs, pb[:, 0:128], AF.Sqrt, bias=eps_c, scale=1.0 / width), [pbb, b_cn], b_scr)
                v_(lambda e: e.reciprocal(rs, rs), b_scr, b_scr)
                for c in range(nchunk):
                    P.op("vector", lambda e, c=c: e.scalar_tensor_tensor(ynT[:, dst_chunk0 + c, tok0:tok0 + 128], yp[:, c, :],
                                                                      gain_ap[:, c:c + 1], rs, ALU.mult, ALU.mult),
                         reads=[b_yp, b_misc, b_sp] + b_scr, writes=[b_ynT])

            xh = TMP[0][:, 0:D]
            if l == 0:
                P.dma("sync", xh, XH, reads=[], writes=[b_tmp[0]])
            else:
                stg = TMP[1][:, 0:D]
                v_(lambda e: e.memset(xh, 0.0), [], [b_tmp[0]])
                for i in range(NCORES):
                    P.dma("sync", stg, XHG[i * 128:(i + 1) * 128, :], reads=[MB("xhg")], writes=[b_tmp[1]])
                    v_(lambda e, i=i: e.scalar_tensor_tensor(xh, stg, cn[:, C_SELH + i:C_SELH + i + 1], xh, ALU.mult, ALU.add),
                       [b_tmp[1], b_tmp[0], b_cn], [b_tmp[0]])
            hTh = TMP[4][:].bitcast(BF16)[:, 0:KC * 128].rearrange("p (k t) -> p k t", t=128)
            norm_block_to_hT(xh, b_tmp[0], gmix, b_gmix, hTh, b_tmp[4], 0)

            def emit_kv(hsrc, b_hsrc, wv, bufs_w, tokcols, slot0, nblk):
                ntok = nblk * 128
                for kv in range(2):
                    pb, pbb = next_bank()

                    def mmk(e, pb=pb, kv=kv):
                        ins = None
                        for kc in range(KC):
                            ins = e.matmul(pb[:, 0:ntok], lhsT=wv[:, kc, kv * 128:(kv + 1) * 128], rhs=hsrc[:, kc, tokcols],
                                           start=(kc == 0), stop=(kc == KC - 1))
                        return ins
                    P.op("tensor", mmk, reads=[b_hsrc] + bufs_w, writes=[pbb])
                    a_(lambda e, pb=pb, kv=kv: e.activation(kT[0:64, kv, 0, slot0:slot0 + nblk, :],
                                                            pb[0:64, 0:ntok].rearrange("p (b t) -> p b t", t=128), AF.Copy),
                       [pbb], [b_kT])
                    v_(lambda e, pb=pb, kv=kv: e.tensor_copy(kT[64:128, kv, 1, slot0:slot0 + nblk, :],
                                                             pb[64:128, 0:ntok].rearrange("p (b t) -> p b t", t=128)),
                       [pbb], [b_kT])
                for tb in range(nblk):
                    pb, pbb = next_bank()
                    t0_ = tokcols.start + tb * 128

                    def mmv2(e, pb=pb, t0_=t0_):
                        ins = None
                        for kc in range(KC):
                            ins = e.matmul(pb[:, 0:128], lhsT=hsrc[:, kc, t0_:t0_ + 128], rhs=wv[:, kc, 256:384],
                                           start=(kc == 0), stop=(kc == KC - 1))
                        return ins
                    P.op("tensor", mmv2, reads=[b_hsrc] + bufs_w, writes=[pbb])
                    src = pb[:, 0:128].rearrange("p (k d) -> p k d", d=64)
                    for dup in range(2):
                        dst = vtok[:, slot0 + tb, :, dup * 64:(dup + 1) * 64]
                        evac(dup, lambda e, src=src, dst=dst: e.activation(dst, src, AF.Copy),
                             lambda e, src=src, dst=dst: e.tensor_copy(dst, src), [pbb], [b_vtok])

            wv, bw_ = wget(hj)
            emit_kv(hTh, b_tmp[4], wv, bw_, slice(0, 128), 0, 1)

            P.fence("vector", fence_fn, b_tmp + [b_wmt, b_bc, b_sloc, b_sin], [b_hT, b_qT] + mixer_bufs + b_act)
            for T in range(NT):
                J = tile_jobs[T]
                tok_base = T * TT
                for tb in range(NB):
                    P.dma("sync", xt[:, tb, :], Xsrc[tok_base + tb * 128:tok_base + (tb + 1) * 128, :], reads=b_xsrc, writes=[b_xt[tb]])
                P.fence("vector", fence_fn, b_E, [b_hT])
                for tb in range(NB):
                    norm_block_to_hT(xt[:, tb, :], b_xt[tb], gmix, b_gmix, hT, b_hT, tb * 128)
                P.fence("vector", fence_fn, b_act, mixer_bufs)
                for i in range(2):
                    wv, bw_ = wget(J["in"][i])
                    for j in range(4):
                        cbk = i * 4 + j
                        pb, pbb = next_bank()

                        def mmq(e, pb=pb, wv=wv, j=j):
                            ins = None
                            for kc in range(KC):
                                ins = e.matmul(pb[:, 0:TT], lhsT=wv[:, kc, j * 128:(j + 1) * 128], rhs=hT[:, kc, :],
                                               start=(kc == 0), stop=(kc == KC - 1))
                            return ins
                        P.op("tensor", mmq, reads=[b_hT] + bw_, writes=[pbb])
                        evac(cbk, lambda e, pb=pb, cbk=cbk: e.activation(qT[:, cbk, :], pb[:, 0:TT], AF.Copy),
                             lambda e, pb=pb, cbk=cbk: e.tensor_copy(qT[:, cbk, :], pb[:, 0:TT]), [pbb], [b_qT])
                wv, bw_ = wget(J["in"][2])
                emit_kv(hT, b_hT, wv, bw_, slice(0, TT), 1, NB)
                wv, bw_ = wget(J["in"][3])
                for j in range(4):
                    pb, pbb = next_bank()

                    def mmu2(e, pb=pb, wv=wv, j=j):
                        ins = None
                        for kc in range(KC):
                            ins = e.matmul(pb[:, 0:TT], lhsT=wv[:, kc, j * 128:(j + 1) * 128], rhs=hT[:, kc, :],
                                           start=(kc == 0), stop=(kc == KC - 1))
                        return ins
                    P.op("tensor", mmu2, reads=[b_hT] + bw_, writes=[pbb])
                    evac(j, lambda e, pb=pb, j=j: e.activation(uT[:, j, :], pb[:, 0:TT], AF.Copy),
                         lambda e, pb=pb, j=j: e.tensor_copy(uT[:, j, :], pb[:, 0:TT]), [pbb], [b_uT])
                wv, bw_ = wget(J["in"][4])
                for j in range(4):
                    pb, pbb = next_bank()

                    def mmzu(e, pb=pb, wv=wv, j=j):
                        ins = None
                        for kc in range(KC):
                            ins = e.matmul(pb[:, 0:TT], lhsT=wv[:, kc, j * 128:(j + 1) * 128], rhs=hT[:, kc, :],
                                           start=(kc == 0), stop=(kc == KC - 1))
                        return ins
                    P.op("tensor", mmzu, reads=[b_hT] + bw_, writes=[pbb])
                    a_(lambda e, pb=pb, j=j: e.activation(guT[:, j, :], pb[:, 0:TT], AF.Gelu), [pbb], [b_guT])
                wv, bw_ = wget(J["in"][5])
                for tb in range(NB):
                    pb, pbb = next_bank()

                    def mmzv(e, pb=pb, wv=wv, tb=tb):
                        ins = None
                        for kc in range(KC):
                            ins = e.matmul(pb[:], lhsT=hT[:, kc, tb * 128:(tb + 1) * 128], rhs=wv[:, kc, :],
                                           start=(kc == 0), stop=(kc == KC - 1))
                        return ins
                    P.op("tensor", mmzv, reads=[b_hT] + bw_, writes=[pbb])
                    gv = ft0[:, 0:512]
                    a_(lambda e, pb=pb: e.activation(gv, pb[:], AF.Gelu, accum_out=st8[:, 0:1]), [pbb], [b_ft[0], b_st8])
                    a_(lambda e: e.activation(ft1[:, 0:512], gv, AF.Square, accum_out=st8[:, 1:2]), [b_ft[0]], [b_ft[1], b_st8])
                    v_(lambda e: e.tensor_scalar(st8[:, 2:3], st8[:, 0:1], 1.0 / 512, None, ALU.mult), [b_st8], [b_st8])
                    v_(lambda e: e.tensor_tensor(st8[:, 3:4], st8[:, 2:3], st8[:, 2:3], ALU.mult), [b_st8], [b_st8])
                    v_(lambda e: e.scalar_tensor_tensor(st8[:, 4:5], st8[:, 1:2], 1.0 / 512, st8[:, 3:4], ALU.mult, ALU.subtract),
                       [b_st8], [b_st8])
                    a_(lambda e: e.activation(st8[:, 5:6], st8[:, 4:5], AF.Sqrt, bias=eps_c), [b_st8, b_cn], [b_st8])
                    v_(lambda e: e.reciprocal(st8[:, 6:7], st8[:, 5:6]), [b_st8], [b_st8])
                    v_(lambda e: e.tensor_scalar(gv, gv, st8[:, 2:3], st8[:, 6:7], ALU.subtract, ALU.mult), [b_ft[0], b_st8], [b_ft[0]])
                    P.op("vector", lambda e: e.tensor_tensor(gv, gv, lngb[:, 0:512], ALU.mult), reads=[b_ft[0], b_misc], writes=[b_ft[0]])
                    P.op("vector", lambda e, tb=tb: e.tensor_tensor(vg[:, tb, :], gv, lngb[:, 512:1024], ALU.add),
                         reads=[b_ft[0], b_misc], writes=[b_vg])
                P.fence("vector", fence_fn, [b_hT], b_E)

                def emit_scores(ks):
                    Eb, bE = EB[ks % 2], b_E[ks % 2]
                    if ks == 0:
                        q0, N, mk, c0 = 0, 128, (mh_b if T == 0 else mcp_b[:, 128:256]), 128
                    elif ks == NB:
                        q0, N, mk, c0 = (NB - 1) * 128, 128, mcp_b[:, 0:128], 0
                    else:
                        q0, N, mk, c0 = (ks - 1) * 128, 256, mcp_b[:, 0:256], 0
                    for hp in range(8):
                        pb, pbb = next_bank()
                        kv = hp // 4

                        def mms(e, pb=pb, hp=hp, kv=kv):
                            ins = None
                            for hh in range(2):
                                e.matmul(pb[:, hh * 256:hh * 256 + N], lhsT=kT[:, kv, hh, ks, :], rhs=qT[:, hp, q0:q0 + N],
                                         start=True, stop=False)
                                ins = e.matmul(pb[:, hh * 256:hh * 256 + N], lhsT=ident_b, rhs=mk, start=False, stop=True)
                            return ins
                        P.op("tensor", mms, reads=[b_kT, b_qT, b_cb], writes=[pbb])
                        a_(lambda e, pb=pb, hp=hp: e.activation(Eb[:, 2 * hp:2 * hp + 2, c0:c0 + N],
                                                                pb[:].rearrange("p (h q) -> p h q", q=256)[:, :, 0:N],
                                                                AF.Exp, scale=HD ** -0.5), [pbb], [bE])

                def emit_pv(n, c_lo, c_hi, do_rms):
                    E0, bE0 = EB[n % 2], b_E[n % 2]
                    E1, bE1 = EB[(n + 1) % 2], b_E[(n + 1) % 2]
                    for c in range(c_lo, c_hi):
                        kv = c // 4
                        pb, pbb = next_bank()

                        def mmpv(e, pb=pb, c=c, kv=kv):
                            ins = None
                            for hh in range(2):
                                h = 2 * c + hh
                                e.matmul(pb[:, hh * 128:(hh + 1) * 128], lhsT=vtok[:, n, kv, :], rhs=E0[:, h, 128:256], start=True, stop=False)
                                e.matmul(pb[:, hh * 128:(hh + 1) * 128], lhsT=vtok[:, n + 1, kv, :], rhs=E1[:, h, 0:128], start=False, stop=True)
                                e.matmul(pb[:, 256 + hh * 128:256 + (hh + 1) * 128], lhsT=ones_b, rhs=E0[:, h, 128:256], start=True, stop=False)
                                ins = e.matmul(pb[:, 256 + hh * 128:256 + (hh + 1) * 128], lhsT=ones_b, rhs=E1[:, h, 0:128], start=False, stop=True)
                            return ins
                        P.op("tensor", mmpv, reads=[b_vtok, bE0, bE1, b_cb], writes=[pbb])
                        for hh in range(2):
                            h = 2 * c + hh
                            a_(lambda e, pb=pb, hh=hh, h=h: e.activation(rr_a[:, hh * 128:(hh + 1) * 128], pb[:, 256 + hh * 128:256 + (hh + 1) * 128],
                                                                         AF.Identity, bias=esink[:, h:h + 1]), [pbb, b_misc], [b_fa])
                        v_(lambda e: e.reciprocal(rr_a, rr_a), [b_fa], [b_fa])
                        v_(lambda e, pb=pb, c=c: e.tensor_tensor(ypre_a[0:64, c, :], pb[0:64, 0:128], rr_a[0:64, 0:128], ALU.mult),
                           [pbb, b_fa], [b_ya])
                        v_(lambda e, pb=pb, c=c: e.tensor_tensor(ypre_a[64:128, c, :], pb[64:128, 128:256], rr_a[64:128, 128:256], ALU.mult),
                           [pbb, b_fa], [b_ya])
                    if do_rms:
                        rms_feature_major(ypre_a, b_ya, sq_a, rs_a, [b_fa], 8, 1024, gatt, 0, n * 128)

                def emit_gmlp(tb):
                    for g in range(4):
                        pb, pbb = next_bank()
                        P.op("tensor", lambda e, pb=pb, g=g: e.matmul(pb[:, 0:128], lhsT=vg[:, tb, g * 128:(g + 1) * 128], rhs=wst[:, g, :],
                                                                    start=True, stop=True), reads=[b_vg, b_wst], writes=[pbb])
                        v_(lambda e, pb=pb, g=g: e.tensor_tensor(tmpm_g, pb[:, 0:128], bsb[:, g * 128:(g + 1) * 128], ALU.add), [pbb, b_misc], [b_fg])
                        v_(lambda e, g=g: e.tensor_tensor(ypre_g[:, g, :], tmpm_g, guT[:, g, tb * 128:(tb + 1) * 128], ALU.mult),
                           [b_fg, b_guT], [b_yg])
                    rms_feature_major(ypre_g, b_yg, sq_g, rs_g, [b_fg], 4, 512, ggm, 12, tb * 128)

                K.ssm = {}

                def ssm_A(tb, half):
                    cg = T * NB + tb
                    ts_ = slice(tb * 128, (tb + 1) * 128)
                    pre = [next_bank() for _ in range(2)]
                    pim = [next_bank() for _ in range(2)]

                    def mmbu(e, pre=pre, pim=pim, half=half, ts_=ts_):
                        ins = None
                        for gl in range(8):
                            gp = half * 8 + gl
                            i = gp // 4
                            e.matmul(pre[gl // 4][0][:, (gl % 4) * 128:(gl % 4 + 1) * 128], lhsT=uT[:, i, ts_], rhs=btb[:, 0, gp, :],
                                     start=True, stop=True)
                            ins = e.matmul(pim[gl // 4][0][:, (gl % 4) * 128:(gl % 4 + 1) * 128], lhsT=uT[:, i, ts_], rhs=btb[:, 1, gp, :],
                                           start=True, stop=True)
                        return ins
                    P.op("tensor", mmbu, reads=[b_uT, b_btb], writes=[pre[0][1], pre[1][1], pim[0][1], pim[1][1]])
                    for q in range(2):
                        gs = slice(half * 8 + q * 4, half * 8 + q * 4 + 4)
                        wre = wm[:, 0, gs, :].rearrange("p g c -> p (g c)")
                        wim = wm[:, 1, gs, :].rearrange("p g c -> p (g c)")
                        pr_, prb_ = pre[q]
                        pi_, pib_ = pim[q]
                        f0 = ft0[:, 0:512]
                        f1 = ft1[:, 0:512]
                        zs = slice(q * 512, (q + 1) * 512)
                        v_(lambda e, pr_=pr_, wre=wre: e.tensor_tensor(f0, pr_[:], wre, ALU.mult), [prb_, b_wm], [b_ft[0]])
                        v_(lambda e, pi_=pi_, wim=wim: e.tensor_tensor(f1, pi_[:], wim, ALU.mult), [pib_, b_wm], [b_ft[1]])
                        P.op("vector", lambda e, zs=zs: e.tensor_tensor(zb[:, 0, zs], f0, f1, ALU.subtract), reads=b_ft, writes=[b_zb])
                        v_(lambda e, pr_=pr_, wim=wim: e.tensor_tensor(f0, pr_[:], wim, ALU.mult), [prb_, b_wm], [b_ft[0]])
                        v_(lambda e, pi_=pi_, wre=wre: e.tensor_tensor(f1, pi_[:], wre, ALU.mult), [pib_, b_wm], [b_ft[1]])
                        P.op("vector", lambda e, zs=zs: e.tensor_tensor(zb[:, 1, zs], f0, f1, ALU.add), reads=b_ft, writes=[b_zb])
                    K.ssm[(tb, half)] = (pre, pim)

                def ssm_B(tb, half):
                    cg = T * NB + tb
                    ts_ = slice(tb * 128, (tb + 1) * 128)
                    pre, pim = K.ssm[(tb, half)]
                    rre = [next_bank() for _ in range(2)]
                    rim = [next_bank() for _ in range(2)]

                    def mmr(e, rre=rre, rim=rim):
                        ins = None
                        for gl in range(8):
                            cs = slice((gl % 4) * 128, (gl % 4 + 1) * 128)
                            e.matmul(rre[gl // 4][0][:, cs], lhsT=zb[:, 0, gl * 128:(gl + 1) * 128], rhs=tri_b, start=True, stop=True)
                            ins = e.matmul(rim[gl // 4][0][:, cs], lhsT=zb[:, 1, gl * 128:(gl + 1) * 128], rhs=tri_b, start=True, stop=True)
                        return ins
                    P.op("tensor", mmr, reads=[b_zb, b_cb], writes=[rre[0][1], rre[1][1], rim[0][1], rim[1][1]])
                    for gl in range(8):
                        gp = half * 8 + gl
                        cs = slice((gl % 4) * 128, (gl % 4 + 1) * 128)
                        rr_, rrb_ = rre[gl // 4]
                        ri_, rib_ = rim[gl // 4]
                        are = asin[:, 0, cg, gp:gp + 1]
                        aim = asin[:, 1, cg, gp:gp + 1]
                        wpr = wpt[:, 0, gp, 0:128]
                        wpi = wpt[:, 1, gp, 0:128]
                        m0 = ft0[:, 0:128] if gl % 2 == 0 else ft0[:, 128:256]
                        m1 = ft1[:, 0:128] if gl % 2 == 0 else ft1[:, 128:256]
                        v_(lambda e, rr_=rr_, cs=cs, are=are, wpr=wpr, m0=m0: e.scalar_tensor_tensor(m0, rr_[:, cs], are, wpr, ALU.add, ALU.mult),
                           [rrb_, b_asin, b_wpt], [b_ft[0]])
                        v_(lambda e, ri_=ri_, cs=cs, aim=aim, wpi=wpi, m1=m1: e.scalar_tensor_tensor(m1, ri_[:, cs], aim, wpi, ALU.add, ALU.mult),
                           [rib_, b_asin, b_wpt], [b_ft[1]])
                        P.op("vector", lambda e, gl=gl, m0=m0, m1=m1: e.tensor_tensor(sbb[:, 0, gl, :], m0, m1, ALU.subtract), reads=b_ft, writes=[b_sbb])
                        v_(lambda e, rr_=rr_, cs=cs, are=are, wpi=wpi, m0=m0: e.scalar_tensor_tensor(m0, rr_[:, cs], are, wpi, ALU.add, ALU.mult),
                           [rrb_, b_asin, b_wpt, b_sbb], [b_ft[0]])
                        v_(lambda e, ri_=ri_, cs=cs, aim=aim, wpr=wpr, m1=m1: e.scalar_tensor_tensor(m1, ri_[:, cs], aim, wpr, ALU.add, ALU.mult),
                           [rib_, b_asin, b_wpt], [b_ft[1]])
                        P.op("vector", lambda e, gl=gl, m0=m0, m1=m1: e.tensor_tensor(sbb[:, 1, gl, :], m0, m1, ALU.add), reads=b_ft, writes=[b_sbb])
                    for ii in range(2):
                        i = half * 2 + ii
                        pb, pbb = next_bank()

                        def mmy(e, pb=pb, ii=ii, i=i, half=half):
                            ins = None
                            for j in range(4):
                                gl = ii * 4 + j
                                gp = half * 8 + gl
                                e.matmul(pb[:, 0:128], lhsT=ctb[:, 0, gp, :], rhs=sbb[:, 0, gl, :], start=(j == 0), stop=False)
                                ins = e.matmul(pb[:, 0:128], lhsT=ctb[:, 1, gp, :], rhs=sbb[:, 1, gl, :], start=False, stop=(j == 3))
                            return ins
                        P.op("tensor", mmy, reads=[b_sbb, b_ctb], writes=[pbb])
                        v_(lambda e, pb=pb, i=i, ts_=ts_: e.scalar_tensor_tensor(ypre[:, 4 + i, :], uT[:, i, ts_], dsk[:, i:i + 1], pb[:, 0:128], ALU.mult, ALU.add),
                           [pbb, b_uT, b_sp], [b_ypre])

                def ssm_tail(tb):
                    a_(lambda e: e.activation(ypre[:, 4:8, :], ypre[:, 4:8, :], AF.Gelu), [b_ypre], [b_ypre])
                    ygb = ft1[:, 512:768].bitcast(BF16).rearrange("p (c t) -> p c t", t=128)
                    v_(lambda e: e.tensor_copy(ygb, ypre[:, 4:8, :]), [b_ypre], [b_ft[1]])
                    for j in range(4):
                        pb, pbb = next_bank()

                        def mmg(e, pb=pb, j=j):
                            ins = None
                            for i in range(4):
                                ins = e.matmul(pb[:, 0:128], lhsT=wglu[:, i, j * 128:(j + 1) * 128], rhs=ygb[:, i, :], start=(i == 0), stop=(i == 3))
                            return ins
                        P.op("tensor", mmg, reads=[b_ft[1], b_wglu], writes=[pbb])
                        sg_ = ft0[:, 640:768]
                        a_(lambda e, pb=pb: e.activation(sg_, pb[:, 0:128], AF.Sigmoid), [pbb], [b_ft[0]])
                        v_(lambda e, j=j: e.tensor_tensor(ypre[:, j, :], ypre[:, 4 + j, :], sg_, ALU.mult), [b_ft[0], b_ypre], [b_ypre])
                    rms_feature_major(ypre, b_ypre, ft1, ft0[:, 0:128], b_ft, 4, 512, gssm, 8, tb * 128)


                emit_scores(0)
                for n in range(NB):
                    emit_scores(n + 1)
                    ssm_A(n, 0)
                    emit_pv(n, 0, 4, False)
                    ssm_B(n, 0)
                    ssm_A(n, 1)
                    emit_pv(n, 4, 8, True)
                    ssm_B(n, 1)
                    ssm_tail(n)
                    emit_gmlp(n)
                if T + 1 < NT:
                    P.op("vector", lambda e: e.tensor_copy(kT[:, :, :, 0, :], kT[:, :, :, NB, :]), reads=[b_kT], writes=[b_kT])
                    P.op("vector", lambda e: e.tensor_copy(vtok[:, 0, :, :], vtok[:, NB, :, :]), reads=[b_vtok], writes=[b_vtok])


                for cb in range(4):
                    wv, bw_ = wget(J["out"][cb])
                    for tb in range(NB):
                        pb, pbb = next_bank()

                        def mmo(e, pb=pb, wv=wv, tb=tb):
                            ins = None
                            for kc in range(KC):
                                ins = e.matmul(pb[:], lhsT=ynT[:, kc, tb * 128:(tb + 1) * 128], rhs=wv[:, kc, :], start=(kc == 0), stop=(kc == KC - 1))
                            return ins
                        P.op("tensor", mmo, reads=[b_ynT] + bw_, writes=[pbb])
                        v_(lambda e, pb=pb, tb=tb, cb=cb: e.tensor_tensor(xt[:, tb, cb * 512:(cb + 1) * 512], xt[:, tb, cb * 512:(cb + 1) * 512], pb[:], ALU.add),
                           [pbb, b_xt[tb]], [b_xt[tb]])

                P.fence("vector", fence_fn, b_E, [b_hT])
                for tb in range(NB):
                    norm_block_to_hT(xt[:, tb, :], b_xt[tb], gffn, b_misc, hT, b_hT, tb * 128)
                P.fence("vector", fence_fn, mixer_bufs, b_act)
                for s_ in range(NST):
                    (jg, ju) = J["gu"][s_]
                    wgv, bwg = wget(jg)
                    wuv, bwu = wget(ju)
                    for hl in range(HPS):
                        hb = s_ * HPS + hl
                        pg, pgb = next_bank()
                        pu, pub = next_bank()

                        def mmgu(e, pg=pg, pu=pu, wgv=wgv, wuv=wuv, hl=hl):
                            ins = None
                            for kc in range(KC):
                                e.matmul(pg[:, 0:TT], lhsT=wgv[:, kc, hl * 128:(hl + 1) * 128], rhs=hT[:, kc, :], start=(kc == 0), stop=(kc == KC - 1))
                            for kc in range(KC):
                                ins = e.matmul(pu[:, 0:TT], lhsT=wuv[:, kc, hl * 128:(hl + 1) * 128], rhs=hT[:, kc, :], start=(kc == 0), stop=(kc == KC - 1))
                            return ins
                        P.op("tensor", mmgu, reads=[b_hT] + bwg + bwu, writes=[pgb, pub])
                        sgv, bsg = sgt[hb % 2][:, 0:TT], b_sgt[hb % 2]
                        a_(lambda e, pg=pg, sgv=sgv: e.activation(sgv, pg[:, 0:TT], AF.Silu), [pgb], [bsg])
                        v_(lambda e, pu=pu, sgv=sgv, hb=hb: e.tensor_tensor(actT[:, hb, :], sgv, pu[:, 0:TT], ALU.mult), [pub, bsg], [b_act[hb]])
                for cb in range(4):
                    accs = [next_bank() for _ in range(NB)]
                    for pi_, (a, b) in enumerate(pieces):
                        wv, bw_ = wget(J["dn"][cb][pi_])
                        for tb in range(NB):
                            def mmd(e, acc=accs[tb][0], wv=wv, tb=tb, a=a, b=b):
                                ins = None
                                for hb in range(a, b):
                                    ins = e.matmul(acc[:], lhsT=actT[:, hb, tb * 128:(tb + 1) * 128], rhs=wv[:, hb - a, :],
                                                   start=(hb == 0), stop=(hb == NHB - 1))
                                return ins
                            P.op("tensor", mmd, reads=b_act[a:b] + bw_, writes=[accs[tb][1]])
                    for tb in range(NB):
                        v_(lambda e, acc=accs[tb][0], tb=tb, cb=cb: e.tensor_tensor(xt[:, tb, cb * 512:(cb + 1) * 512], xt[:, tb, cb * 512:(cb + 1) * 512], acc[:], ALU.add),
                           [accs[tb][1], b_xt[tb]], [b_xt[tb]])
                for tb in range(NB):
                    if final_norm:
                        gfb = qTflat.bitcast(F32)[:, 0:D]
                        if tb == 0:
                            P.dma("sync", gfb, GFIN.partition_broadcast(128), reads=[], writes=[b_qT])
                        a_(lambda e, tb=tb: e.activation(xn[:], xt[:, tb, :], AF.Square, accum_out=nrm[:, 4:5]), [b_xt[tb]], [b_xn, b_nrm])
                        a_(lambda e: e.activation(nrm[:, 5:6], nrm[:, 4:5], AF.Sqrt, bias=eps_c, scale=1.0 / D), [b_nrm, b_cn], [b_nrm])
                        v_(lambda e: e.reciprocal(nrm[:, 6:7], nrm[:, 5:6]), [b_nrm], [b_nrm])
                        v_(lambda e, tb=tb: e.scalar_tensor_tensor(xt[:, tb, :], xt[:, tb, :], nrm[:, 6:7], gfb, ALU.mult, ALU.mult),
                           [b_xt[tb], b_nrm, b_qT], [b_xt[tb]])
                    P.dma("sync", Xdst[tok_base + tb * 128:tok_base + (tb + 1) * 128, :], xt[:, tb, :], reads=[b_xt[tb]],
                          writes=b_xdst, is_output=final_norm)


            if not final_norm:
                P.dma("sync", XHB, X1[NTOK - 128:NTOK, :], reads=[MB("x1d")], writes=[MB("xhb")])
                P.coll(lambda e: e.collective_compute("AllGather", ALU.bypass, replica_groups=[list(range(NCORES))],
                                                      ins=[XHB.opt()], outs=[XHG.opt()]),
                       reads=[MB("xhb")], writes=[MB("xhg")])

        emit_casts(0)
        for l in range(depth):
            emit_layer(l)
        P.finish()
        with nc.Block() as block:
            P.emit(block)
    return nc


def _ch_major(a):
    return np.ascontiguousarray(a.reshape(16, 2, 64).transpose(1, 2, 0).reshape(128, 16))


def _layer_small(inp, l):
    f = np.float32
    sp = np.zeros((128, 64), f)
    sp[:, 0:16] = _ch_major(inp["ssm_lam_re"][l])
    sp[:, 16:32] = _ch_major(inp["ssm_lam_im"][l])
    sp[:, 32:48] = _ch_major(np.repeat(inp["ssm_log_dt"][l][:, None], 64, axis=1))
    sp[:, 48:52] = inp["ssm_d"][l].reshape(4, 128).T
    sp[:, 52:56] = inp["out_norm_ssm"][l].reshape(4, 128).T
    sp[:, 56:60] = inp["out_norm_gmlp"][l].reshape(4, 128).T
    b32 = np.zeros((128, 2, 16, 32), f)
    ctp = np.zeros((128, 2, 16, 128), f)
    for ri, (bk, ck) in enumerate((("ssm_b_re", "ssm_c_re"), ("ssm_b_im", "ssm_c_im"))):
        Bm = inp[bk][l].reshape(16, 2, 64, 16)
        Cm = inp[ck][l].reshape(16, 2, 16, 64)
        for g2 in range(2):
            b32[g2 * 64:(g2 + 1) * 64, ri, :, g2 * 16:(g2 + 1) * 16] = Bm[:, g2].transpose(1, 0, 2)
            for gp in range(16):
                col = ((gp % 4) * 2 + g2) * 16
                ctp[g2 * 64:(g2 + 1) * 64, ri, gp, col:col + 16] = Cm[gp, g2].T
    return dict(sp=sp, b32=b32, ctp=ctp,
                gmix=np.ascontiguousarray(inp["norm_mix"][l].reshape(KC, 128).T))


def kernel(**inp):
    inp = {k: np.asarray(v) for k, v in inp.items()}
    x = inp["x"]
    Bsz, L, _ = x.shape
    depth = inp["w_in"].shape[0]
    DFF = inp["w_gate"].shape[2]
    ncores = NCORES
    cps = ncores // Bsz
    NTOK = L // cps
    xs = np.ascontiguousarray(x.reshape(ncores, NTOK, D))
    prog = build_fused(NTOK, DFF, depth)
    shared = dict(gfin=np.ascontiguousarray(inp["norm_final"][None, :]))
    for l in range(depth):
        sm = _layer_small(inp, l)
        sfx = "_%d" % l
        lay = dict(
            sp=sm["sp"], b32=sm["b32"], gmix=sm["gmix"], ctp=sm["ctp"],
            w_in=inp["w_in"][l], w_out=inp["w_out"][l], w_gate=inp["w_gate"][l], w_up=inp["w_up"][l], w_down=inp["w_down"][l],
            w_glu=inp["ssm_w_glu"][l],
            ws=np.ascontiguousarray(inp["gmlp_w_s"][l].transpose(1, 0, 2)),
            bs=np.ascontiguousarray(inp["gmlp_b_s"][l].reshape(1, 512)),
            lngb=np.ascontiguousarray(np.concatenate([inp["gmlp_ln_g"][l], inp["gmlp_ln_b"][l]])[None, :]),
            sinks=np.ascontiguousarray(inp["attn_sinks"][l][None, :]),
            gatt=np.ascontiguousarray(inp["out_norm_attn"][l].reshape(8, 128).T),
            gffn=np.ascontiguousarray(inp["norm_ffn"][l].reshape(KC, 128).T),
        )
        for k, v in lay.items():
            shared[k + sfx] = v
    in_maps = []
    for c in range(ncores):
        if c % cps == 0:
            xh = np.zeros((128, D), np.float32)
        else:
            xh = np.ascontiguousarray(xs[c - 1][NTOK - 128:NTOK])
        m = dict(x=xs[c], xh=xh, consts=host_consts(c, ncores, cps))
        m.update(shared)
        in_maps.append(m)
    res = run_bass_kernel_spmd(prog, in_maps, core_ids=list(range(ncores))).results
    out = np.stack([r["xo"] for r in res], axis=0)
    return np.ascontiguousarray(out.reshape(Bsz, L, D)).astype(np.float32, copy=False)
```

```python
import math
from contextlib import ExitStack

import numpy as np
import concourse.bass as bass
import concourse.mybir as mybir
from concourse.bass_utils import run_bass_kernel_spmd

F32 = mybir.dt.float32
BF16 = mybir.dt.bfloat16
I32 = mybir.dt.int32
AF = mybir.ActivationFunctionType
ALU = mybir.AluOpType
AX = mybir.AxisListType

NCORES = 8
DBG_STOP = 0
D = 2048
KC = 16
NQH = 16
HD = 64
DIN = 2816
DINP = 3072
EPS = 1e-5
NEG = -30000.0
TWO_PI = 2.0 * math.pi


class Buf:
    def __init__(self, name="", exclusive=False):
        self.name = name
        self.exclusive = exclusive
        self.writers = []
        self.readers = []
        self.deps = []
        self.state = "r"


class Stream:
    def __init__(self, name):
        self.name = name
        self.ops = []
        self.sem = None
        self.count = 0
        self.known = {}


SEM_ROT = 30000


class Prog:
    def __init__(self, nc, n_dma_sems=24):
        self.nc = nc
        self.streams = {n: Stream(n) for n in ["tensor", "vector", "scalar", "gpsimd", "sync"]}
        self.n_dma_sems = n_dma_sems
        self.dma_sems = []
        self.dma_rr = 0
        self.stack = None
        self.nsem = 0
        self.out_tokens = []
        self.coll_tokens = []
        self.stopped = False

    def new_sem(self):
        s = self.stack.enter_context(self.nc.semaphore("s%d" % self.nsem))
        self.nsem += 1
        return s

    def begin(self, stack):
        self.stack = stack
        for st in self.streams.values():
            st.sem = self.new_sem()
        self.dma_sems = [[self.new_sem(), 0] for _ in range(self.n_dma_sems)]
        self.dma_pools = {"sync": self.dma_sems[:16], "gpsimd": self.dma_sems[16:], "scalar": self.dma_sems[16:]}
        self.dma_rr = {"sync": 0, "gpsimd": 0, "scalar": 0}

    def _collect(self, reads, writes, st=None):
        toks = []
        for b in reads:
            toks += b.writers
            if b.exclusive and st is not None:
                toks += [t for t in b.readers if t[0] is not st.sem]
        for b in writes:
            if b.state == "r":
                toks += b.readers + b.writers
            else:
                toks += b.deps
                if b in reads:
                    toks += b.writers
        return toks

    def _commit(self, tok, reads, writes):
        for b in writes:
            if b.state == "r":
                b.deps = b.readers + b.writers
                b.readers = []
                b.writers = [tok]
                b.state = "w"
            else:
                if b in reads:
                    b.deps = b.deps + b.writers
                    b.writers = [tok]
                else:
                    b.writers.append(tok)
        for b in reads:
            if b in writes:
                continue
            b.readers.append(tok)
            b.state = "r"

    def _waits(self, st, toks):
        need = {}
        for sem, val in toks:
            k = id(sem)
            if st.known.get(k, 0) >= val:
                continue
            if k not in need or need[k][1] < val:
                need[k] = (sem, val)
        out = []
        for k, (sem, val) in need.items():
            st.known[k] = val
            out.append((sem, val))
        return out

    def op(self, eng, fn, reads=(), writes=()):
        if self.stopped:
            return None
        st = self.streams[eng]
        reads = list(reads)
        writes = list(writes)
        waits = self._waits(st, self._collect(reads, writes, st))
        if st.count >= SEM_ROT:
            st.sem = self.new_sem()
            st.count = 0
        st.count += 1
        tok = (st.sem, st.count)
        st.ops.append((waits, fn, (st.sem, 1)))
        self._commit(tok, reads, writes)
        return tok

    def fence(self, eng, fn, old, new):
        bufs = []
        for b in list(old) + list(new):
            if b not in bufs:
                bufs.append(b)
        tok = self.op(eng, fn, reads=(), writes=bufs)
        if tok is None:
            return None
        for b in bufs:
            b.readers.append(tok)
            b.state = "r"
        return tok

    def dma(self, queue, out_ap, in_ap, reads=(), writes=(), is_output=False, **kw):
        if self.stopped:
            return None
        st = self.streams[queue]
        reads = list(reads)
        writes = list(writes)
        pool = self.dma_pools[queue]
        ent = pool[self.dma_rr[queue] % len(pool)]
        self.dma_rr[queue] += 1
        toks = self._collect(reads, writes, st)
        if ent[1] > 0:
            toks.append((ent[0], ent[1] * 16))
        if ent[1] >= SEM_ROT // 16:
            ent[0] = self.new_sem()
            ent[1] = 0
        waits = self._waits(st, toks)
        ent[1] += 1
        tok = (ent[0], ent[1] * 16)

        def fn(e, out_ap=out_ap, in_ap=in_ap, kw=kw):
            return e.dma_start(out=out_ap, in_=in_ap, **kw)

        st.ops.append((waits, fn, (ent[0], 16)))
        self._commit(tok, reads, writes)
        if is_output:
            self.out_tokens.append(tok)
        return tok

    def coll(self, fn, reads=(), writes=()):
        st = self.streams["gpsimd"]
        reads = list(reads)
        writes = list(writes)
        waits = self._waits(st, self._collect(reads, writes, st))
        sem = self.new_sem()
        tok = (sem, 1)
        st.ops.append((waits, fn, (sem, 1)))
        self._commit(tok, reads, writes)
        self.coll_tokens.append(tok)
        return tok

    def finish(self):
        st = self.streams["sync"]
        toks = list(self.out_tokens) + list(self.coll_tokens)
        for ent in self.dma_sems:
            if ent[1] > 0:
                toks.append((ent[0], ent[1] * 16))
        for n, s2 in self.streams.items():
            if s2.count > 0:
                toks.append((s2.sem, s2.count))
        waits = self._waits(st, toks)
        st.ops.append((waits, None, None))

    def emit(self, block):
        def replay(st):
            def run(e):
                for waits, fn, inc in st.ops:
                    for sem, val in waits:
                        e.wait_ge(sem, val)
                    if fn is not None:
                        ins = fn(e)
                        ins.then_inc(inc[0], inc[1])
            return run

        block.tensor(replay(self.streams["tensor"]))
        block.vector(replay(self.streams["vector"]))
        block.scalar(replay(self.streams["scalar"]))
        block.gpsimd(replay(self.streams["gpsimd"]))
        block.sync(replay(self.streams["sync"]))


C_ID = 0
C_TRI = 128
C_TV = 256
C_SEL = 385
C_EPS = 409
C_SELH = 410
NPERS = 418
C_MCP = 418
C_MH = 674
C_ONE = 802
NCONST = 930


def host_consts(core, ncores, cps):
    c = np.zeros((128, NCONST), np.float32)
    c[:, C_ID:C_ID + 128] = np.eye(128, dtype=np.float32)
    s = np.arange(128)[:, None]
    t = np.arange(128)[None, :]
    c[:, C_TRI:C_TRI + 128] = (s <= t)
    c[:, C_MCP:C_MCP + 128] = np.where(s <= t, 0.0, NEG)
    c[:, C_MCP + 128:C_MCP + 256] = np.where(s > t, 0.0, NEG)
    c[:, C_TV:C_TV + 129] = np.arange(129, dtype=np.float32)[None, :]
    j = core % cps
    if j == 0:
        c[:, C_MH:C_MH + 128] = NEG
    else:
        c[:, C_MH:C_MH + 128] = np.where(s > t, 0.0, NEG)
    base = core - j
    for i in range(j):
        m = j - 1 - i
        c[:, C_SEL + (base + i) * 3 + m] = 1.0
    if j > 0:
        c[:, C_SELH + core - 1] = 1.0
    c[:, C_ONE:C_ONE + 128] = 1.0
    c[:, C_EPS] = EPS
    return c


class Ctx:
    pass


def build_fused(NTOK, DFF, depth):
    NCH = NTOK // 128
    TT = min(512, NTOK)
    NT = NTOK // TT
    NB = TT // 128
    NHB = DFF // 128
    nc = bass.Bass("TRN2", target_bir_lowering=False)
    K = Ctx()
    K.nc = nc
    dt_in = lambda name, shape, dt=F32: nc.dram_tensor(name, shape, dt, kind="ExternalInput").ap()
    dt_out = lambda name, shape, dt=F32: nc.dram_tensor(name, shape, dt, kind="ExternalOutput").ap()
    dt_int = lambda name, shape, dt=F32: nc.dram_tensor(name, shape, dt, kind="Internal").ap()

    X = dt_in("x", [NTOK, D])
    XH = dt_in("xh", [128, D])
    CN = dt_in("consts", [128, NCONST])
    GFIN = dt_in("gfin", [1, D])
    XO = dt_out("xo", [NTOK, D])
    X1 = dt_int("x1", [NTOK, D])
    XHB = dt_int("xhb", [128, D])
    XHG = dt_int("xhg", [NCORES * 128, D])
    LW, LB, SENDB, SENDG = [], [], [], []
    for l in range(depth):
        sfx = "_%d" % l
        LW.append(dict(
            sp=dt_in("sp" + sfx, [128, 64]), b32=dt_in("b32" + sfx, [128, 2, 16, 32]), gmix=dt_in("gmix" + sfx, [128, KC]),
            w_in=dt_in("w_in" + sfx, [D, DIN]), w_out=dt_in("w_out" + sfx, [D, D]), w_gate=dt_in("w_gate" + sfx, [D, DFF]),
            w_up=dt_in("w_up" + sfx, [D, DFF]), w_down=dt_in("w_down" + sfx, [DFF, D]), w_glu=dt_in("w_glu" + sfx, [512, 512]),
            ctp=dt_in("ctp" + sfx, [128, 2, 16, 128]), ws=dt_in("ws" + sfx, [128, 4, 128]), bs=dt_in("bs" + sfx, [1, 512]),
            lngb=dt_in("lngb" + sfx, [1, 1024]), sinks=dt_in("sinks" + sfx, [1, 16]), gatt=dt_in("gatt" + sfx, [128, 8]),
            gffn=dt_in("gffn" + sfx, [128, KC])))
        LB.append((dt_int("winb" + sfx, [D, DINP], BF16), dt_int("woutb" + sfx, [D, D], BF16), dt_int("wgb" + sfx, [D, DFF], BF16),
                   dt_int("wub" + sfx, [D, DFF], BF16), dt_int("wdb" + sfx, [DFF, D], BF16)))
        SENDB.append(dt_int("sendb" + sfx, [128, 32]))
        SENDG.append(dt_int("sendg" + sfx, [NCORES * 128, 32]))

    with ExitStack() as es:
        P = Prog(nc)
        P.begin(es)
        K.P = P

        K.sbs = {}
        K.bufs = {}

        def sb(name, shape, dt=F32):
            if name not in K.sbs:
                K.sbs[name] = es.enter_context(nc.sbuf_tensor("sb_" + name, shape, dt))
            return K.sbs[name]

        def MB(name, exclusive=False):
            if name not in K.bufs:
                K.bufs[name] = Buf(name, exclusive)
            return K.bufs[name]

        def psum(name, shape, dt=F32):
            return es.enter_context(nc.psum_tensor("ps_" + name, shape, dt))

        NBK = 6
        banks = [psum("bank%d" % i, [128, 512], F32) for i in range(NBK)]
        bankb = [MB("bank%d" % i, exclusive=True) for i in range(NBK)]
        tbanks = [psum("tbank%d" % i, [128, 1024], BF16) for i in range(2)]
        tbankbs = [MB("tbank%d" % i, exclusive=True) for i in range(2)]
        K.rr = 0
        K.trr = 0

        def next_bank():
            i = K.rr
            K.rr = (K.rr + 1) % NBK
            return banks[i], bankb[i]

        def next_tbank():
            i = K.trr
            K.trr = (K.trr + 1) % 2
            return tbanks[i], tbankbs[i]

        cn = sb("cn", [128, NPERS])
        b_cn = MB("cn")
        P.dma("sync", cn[:], CN[:, 0:NPERS], writes=[b_cn])
        cb16 = sb("cb16", [128, 768], BF16)
        b_cb = MB("cb16")
        P.op("vector", lambda e: e.tensor_copy(cb16[:, 0:256], cn[:, 0:256]), reads=[b_cn], writes=[b_cb])
        ident_f = cn[:, C_ID:C_ID + 128]
        tri_f = cn[:, C_TRI:C_TRI + 128]
        tvals = cn[:, C_TV:C_TV + 129]
        eps_c = cn[:, C_EPS:C_EPS + 1]
        ident_b = cb16[:, 0:128]
        tri_b = cb16[:, 128:256]
        mcp_b = cb16[:, 256:512]
        mh_b = cb16[:, 512:640]
        ones_b = cb16[:, 640:768]

        NCHK = NCH
        NTAB = 16 * 129
        b_tmp = [MB("tmp%d" % i) for i in range(5)]
        NBIG = max(NHB, 44) * TT
        big = sb("big", [128, NBIG], BF16)
        bigf = big[:].bitcast(F32)
        TMP = [bigf[:, i * NTAB:(i + 1) * NTAB] for i in range(5)]
        hT = sb("hT", [128, KC, TT], BF16)
        qT = sb("qT", [128, 8, TT], BF16)
        b_wmt, b_bc, b_sloc, b_sin = MB("wmt"), MB("bc"), MB("sloc"), MB("sin")
        xt = sb("xt", [128, NB, D])
        b_xt = [MB("xt%d" % i) for i in range(NB)]
        b_hT = MB("hT")
        EB = [hT[:].rearrange("p k t -> p (k t)")[:, i * 4096:(i + 1) * 4096].rearrange("p (h q) -> p h q", q=256) for i in range(2)]
        b_E = [MB("E0"), MB("E1")]
        wp = sb("wp", [128, 4, 4096], BF16)
        b_wp = [MB("wp%d" % i) for i in range(4)]
        b_qT = MB("qT")
        kT = sb("kT", [128, 2, 2, NB + 1, 128], BF16)
        b_kT = MB("kT")
        P.op("vector", lambda e: e.memset(kT[:], 0.0), reads=[], writes=[b_kT])
        b_kT.readers.append(b_kT.writers[-1])
        b_kT.state = "r"
        vtok = sb("vtok", [128, NB + 1, 2, 128], BF16)
        b_vtok = MB("vtok")
        actT = big[:, 0:NHB * TT].rearrange("p (h t) -> p h t", t=TT)
        b_act = [MB("act%d" % i) for i in range(NHB)]
        o = [0]

        def carve(n_bf16):
            a = o[0]
            o[0] += n_bf16
            return big[:, a:a + n_bf16]
        ynT = carve(16 * TT).rearrange("p (c t) -> p c t", t=TT)
        b_ynT = MB("ynT")
        uT = carve(4 * TT).rearrange("p (c t) -> p c t", t=TT)
        b_uT = MB("uT")
        guT = carve(4 * TT).rearrange("p (c t) -> p c t", t=TT)
        b_guT = MB("guT")
        vg = carve(4 * TT).rearrange("p (c t) -> p c t", t=512)
        b_vg = MB("vg")
        ypre = carve(2 * 1024).bitcast(F32).rearrange("p (c t) -> p c t", t=128)
        b_ypre = MB("ypre")
        ft0 = carve(2 * 1024).bitcast(F32)
        ft1 = carve(2 * 1024).bitcast(F32)
        b_ft = [MB("ft0"), MB("ft1")]
        ypre_a = carve(2 * 1024).bitcast(F32).rearrange("p (c t) -> p c t", t=128)
        b_ya = MB("ypre_a")
        assert o[0] <= NBIG, o[0]
        zb = sb("zb", [128, 2, 1024], BF16)
        b_zb = MB("zb")
        sbb = sb("sbb", [128, 2, 8, 128], BF16)
        b_sbb = MB("sbb")
        sgt = [zb[:].rearrange("p r c -> p (r c)").bitcast(F32), sbb[:].rearrange("p r g c -> p (r g c)").bitcast(F32)]
        b_sgt = [b_zb, b_sbb]
        st8 = sb("st8", [128, 16])
        b_st8 = MB("st8")
        mixer_bufs = [b_ynT, b_uT, b_guT, b_vg, b_ypre, b_ya] + b_ft
        fsc = sb("fsc", [128, 8])
        b_fsc = MB("fsc")

        def fence_fn(e):
            return e.memset(fsc[:], 0.0)
        xs2 = sb("xs2", [128, 640])
        rr_a = xs2[:, 0:256]
        sq_a = xs2[:, 0:512]
        rs_a = xs2[:, 512:640]
        b_fa = MB("fta")
        xnf = sb("xn", [128, D], BF16)[:].bitcast(F32)
        ypre_g = xnf[:, 0:512].rearrange("p (c t) -> p c t", t=128)
        tmpm_g = xnf[:, 512:640]
        sq_g = xnf[:, 512:768]
        rs_g = xnf[:, 768:896]
        b_yg = b_fg = MB("xn")
        ctr = xt[:, 0, 0:512]
        P.dma("sync", ctr, CN[:, NPERS:NCONST], writes=[b_xt[0]])
        P.op("vector", lambda e: e.tensor_copy(cb16[:, 256:768], ctr), reads=[b_xt[0]], writes=[b_cb])
        ALL_MAIN = [b_hT, b_qT] + mixer_bufs + b_act + b_E
        ALL_SETUP = b_tmp + [b_wmt, b_bc, b_sloc, b_sin]


        PIECES = [(a_, min(NHB, a_ + 16)) for a_ in range(0, NHB, 16)]

        def emit_casts(l):
            WIN, WOUT, WG, WU, WD = LW[l]["w_in"], LW[l]["w_out"], LW[l]["w_gate"], LW[l]["w_up"], LW[l]["w_down"]
            WINB, WOUTB, WGB, WUB, WDB = LB[l]
            b_winb, b_woutb, b_wgb, b_wub, b_wdb = [MB(n + str(l)) for n in ("winb", "woutb", "wgb", "wub", "wdb")]
            RB = 256
            b_winb_s = MB("winbs%d" % l)
            for r0 in range(0, D, RB):
                rs = slice(r0, r0 + RB)
                P.dma("gpsimd", WINB[rs, 1536:3072], WIN[rs, 1280:2816], writes=[b_winb_s])
            for r0 in range(0, D, RB):
                rs = slice(r0, r0 + RB)
                P.dma("gpsimd", WINB[rs, 0:1024], WIN[rs, 0:1024], writes=[b_winb])
                for kv in range(2):
                    for dup in range(2):
                        c0 = 1024 + kv * 128 + dup * 64
                        P.dma("gpsimd", WINB[rs, c0:c0 + 64], WIN[rs, 1024 + kv * 64:1024 + kv * 64 + 64], writes=[b_winb])
                P.dma("gpsimd", WINB[rs, 1280:1408], WIN[rs, 1152:1280], writes=[b_winb])
                P.dma("gpsimd", WINB[rs, 1408:1536], WIN[rs, 1152:1280], writes=[b_winb])
            for r0 in range(0, D, RB):
                rs = slice(r0, r0 + RB)
                P.dma("gpsimd", WOUTB[rs, :], WOUT[rs, :], writes=[b_woutb])
            for s_ in range(NHB // 2):
                cs = slice(s_ * 256, (s_ + 1) * 256)
                for r0 in range(0, D, 512):
                    rs = slice(r0, r0 + 512)
                    P.dma("gpsimd", WGB[rs, cs], WG[rs, cs], writes=[MB("wgb%d_%d" % (l, s_))])
                    P.dma("gpsimd", WUB[rs, cs], WU[rs, cs], writes=[MB("wub%d_%d" % (l, s_))])
            for pi_, (pa, pb_) in enumerate(PIECES):
                for r0 in range(pa * 128, pb_ * 128, RB):
                    rs = slice(r0, min(pb_ * 128, r0 + RB))
                    P.dma("gpsimd", WDB[rs, :], WD[rs, :], writes=[MB("wdb%d_%d" % (l, pi_))])


        def emit_layer(l):
            final_norm = (l == depth - 1)
            SP, B32, GMIX = LW[l]["sp"], LW[l]["b32"], LW[l]["gmix"]
            WGLU, CTP, WS, BS, LNGB, SINK, GATT, GFFN = [LW[l][k] for k in ("w_glu", "ctp", "ws", "bs", "lngb", "sinks", "gatt", "gffn")]
            WINB, WOUTB, WGB, WUB, WDB = LB[l]
            b_winb, b_woutb, b_wgb, b_wub, b_wdb = [MB(n + str(l)) for n in ("winb", "woutb", "wgb", "wub", "wdb")]
            WINBv = WINB.rearrange("(kc p) n -> p kc n", p=128)
            Xsrc = X if l == 0 else X1
            b_xsrc = [] if l == 0 else [MB("x1d")]
            Xdst = XO if final_norm else X1
            b_xdst = [] if final_norm else [MB("x1d")]
            P.fence("vector", fence_fn, ALL_MAIN, ALL_SETUP)
            spt = sb("spt", [128, 64])
            b_sp = MB("sp")
            P.dma("sync", spt[:], SP, writes=[b_sp])
            gmix = sb("gmixt", [128, KC])
            b_gmix = MB("gmix")
            P.dma("sync", gmix[:], GMIX, writes=[b_gmix])

            NTAB = 16 * 129
            b_tmp = [MB("tmp%d" % i) for i in range(5)]
            NBIG = max(NHB, 44) * TT
            big = sb("big", [128, NBIG], BF16)
            bigf = big[:].bitcast(F32)
            TMP = [bigf[:, i * NTAB:(i + 1) * NTAB] for i in range(5)]
            hT = sb("hT", [128, KC, TT], BF16)
            hTflat = hT[:].rearrange("p k t -> p (k t)")
            qT = sb("qT", [128, 8, TT], BF16)
            qTflat = qT[:].rearrange("p k t -> p (k t)")
            TI = TMP[2][:].bitcast(I32)

            wm = sb("wm", [128, 2, 16, 128], BF16)
            b_wm = MB("wm")
            wpt = sb("wpt", [128, 2, 16, 129], BF16)
            b_wpt = MB("wpt")
            wmt = hTflat[:, 0:4096].rearrange("p (r g c) -> p r g c", r=2, g=16)
            bc = hTflat[:, 4096:6144].bitcast(F32).rearrange("p (r g c) -> p r g c", r=2, g=16)
            b_wmt = MB("wmt")
            sm = sb("sm", [128, 24, 16])
            b_sm = MB("sm")
            b_bc = MB("bc")

            SMI = dict(dt=0, rho=1, th=2, t0=3, t1=4, thr=5, a_re=6, a_im=7, a127r=8, a127i=9, a128r=10, a128i=11,
                       cr=12, ci=13, den=14, t2=15, t3=16, pnr=17, pni=18, p2r=19, p2i=20, t4=21, t5=22, rden=23)

            def S_(n):
                return sm[:, SMI[n], :]

            lr = spt[:, 0:16]
            li = spt[:, 16:32]
            ldt = spt[:, 32:48]

            def v_(fn, reads, writes):
                return P.op("vector", fn, reads=reads, writes=writes)

            def a_(fn, reads, writes):
                return P.op("scalar", fn, reads=reads, writes=writes)

            a_(lambda e: e.activation(S_("dt"), ldt, AF.Exp), [b_sp], [b_sm])
            v_(lambda e: e.tensor_tensor(S_("rho"), lr, S_("dt"), ALU.mult), [b_sp, b_sm], [b_sm])
            v_(lambda e: e.tensor_tensor(S_("th"), li, S_("dt"), ALU.mult), [b_sp, b_sm], [b_sm])
            smi = sm[:, SMI["t1"], :].bitcast(I32)
            v_(lambda e: e.tensor_scalar(S_("t0"), S_("th"), 1.0 / TWO_PI, None, ALU.mult), [b_sm], [b_sm])
            v_(lambda e: e.tensor_copy(smi, S_("t0")), [b_sm], [b_sm])
            v_(lambda e: e.tensor_copy(S_("t0"), smi), [b_sm], [b_sm])
            v_(lambda e: e.scalar_tensor_tensor(S_("thr"), S_("t0"), -TWO_PI, S_("th"), ALU.mult, ALU.add), [b_sm], [b_sm])

            T3 = lambda i: TMP[i][:].rearrange("p (g t) -> p g t", t=129)
            bc3 = lambda ap: ap.unsqueeze(2).to_broadcast([128, 16, 129])
            tv3 = tvals.unsqueeze(1).to_broadcast([128, 16, 129])

            def reduce_angle(src, dst):
                v_(lambda e: e.tensor_scalar(TMP[1][:], TMP[src][:], 1.0 / TWO_PI, None, ALU.mult), [b_tmp[src]], [b_tmp[1]])
                v_(lambda e: e.tensor_copy(TI, TMP[1][:]), [b_tmp[1]], [b_tmp[2]])
                v_(lambda e: e.tensor_copy(TMP[1][:], TI), [b_tmp[2]], [b_tmp[1]])
                v_(lambda e: e.scalar_tensor_tensor(TMP[dst][:], TMP[1][:], -TWO_PI, TMP[src][:], ALU.mult, ALU.add),
                   [b_tmp[1], b_tmp[src]], [b_tmp[dst]])

            v_(lambda e: e.tensor_tensor(T3(0), bc3(S_("thr")), tv3, ALU.mult), [b_sm, b_cn], [b_tmp[0]])
            reduce_angle(0, 3)
            a_(lambda e: e.activation(TMP[3][:], TMP[3][:], AF.Sin), [b_tmp[3]], [b_tmp[3]])
            v_(lambda e: e.tensor_scalar(TMP[0][:], TMP[0][:], math.pi / 2, None, ALU.add), [b_tmp[0]], [b_tmp[0]])
            reduce_angle(0, 4)
            a_(lambda e: e.activation(TMP[4][:], TMP[4][:], AF.Sin), [b_tmp[4]], [b_tmp[4]])
            v_(lambda e: e.tensor_tensor(T3(0), bc3(S_("rho")), tv3, ALU.mult), [b_sm, b_cn], [b_tmp[0]])
            a_(lambda e: e.activation(TMP[1][:], TMP[0][:], AF.Exp, scale=-1.0), [b_tmp[0]], [b_tmp[1]])
            a_(lambda e: e.activation(TMP[0][:], TMP[0][:], AF.Exp), [b_tmp[0]], [b_tmp[0]])
            v_(lambda e: e.tensor_tensor(wmt[:, 0, :, :], T3(1)[:, :, 0:128], T3(4)[:, :, 0:128], ALU.mult),
               [b_tmp[1], b_tmp[4]], [b_wmt])
            v_(lambda e: e.scalar_tensor_tensor(wmt[:, 1, :, :], T3(1)[:, :, 0:128], -1.0, T3(3)[:, :, 0:128], ALU.mult, ALU.mult),
               [b_tmp[1], b_tmp[3]], [b_wmt])
            v_(lambda e: e.tensor_tensor(TMP[1][:], TMP[0][:], TMP[4][:], ALU.mult), [b_tmp[0], b_tmp[4], b_wmt], [b_tmp[1]])
            v_(lambda e: e.tensor_tensor(TMP[2][:], TMP[0][:], TMP[3][:], ALU.mult), [b_tmp[0], b_tmp[3]], [b_tmp[2]])
            v_(lambda e: e.tensor_copy(wpt[:, 0, :, :], T3(1)), [b_tmp[1]], [b_wpt])
            v_(lambda e: e.tensor_copy(wpt[:, 1, :, :], T3(2)), [b_tmp[2]], [b_wpt])
            for nm, src, col in (("a_re", 1, 1), ("a_im", 2, 1), ("a127r", 1, 127), ("a127i", 2, 127),
                                 ("a128r", 1, 128), ("a128i", 2, 128)):
                v_(lambda e, nm=nm, src=src, col=col: e.tensor_copy(S_(nm), T3(src)[:, :, col]), [b_tmp[src]], [b_sm])
            for ri in range(2):
                for g4 in range(4):
                    tbank, tbankb = next_tbank()

                    def tr(e, ri=ri, g4=g4, tbank=tbank):
                        ins = None
                        for j in range(4):
                            ins = e.transpose(tbank[:, j * 128:(j + 1) * 128], wmt[:, ri, g4 * 4 + j, :], ident_b)
                        return ins
                    P.op("tensor", tr, reads=[b_wmt, b_cb], writes=[tbankb])
                    v_(lambda e, ri=ri, g4=g4, tbank=tbank: e.tensor_copy(wm[:, ri, g4 * 4:(g4 + 1) * 4, :],
                                                                          tbank[:, 0:512].rearrange("p (j c) -> p j c", c=128)),
                       [tbankb], [b_wm])

            def cmul(outr, outi, ar, ai, br, bi, t0="t2", t1="t3"):
                v_(lambda e: e.tensor_tensor(S_(t0), S_(ar), S_(br), ALU.mult), [b_sm], [b_sm])
                v_(lambda e: e.tensor_tensor(S_(t1), S_(ai), S_(bi), ALU.mult), [b_sm], [b_sm])
                v_(lambda e: e.tensor_tensor(S_(outr), S_(t0), S_(t1), ALU.subtract), [b_sm], [b_sm])
                v_(lambda e: e.tensor_tensor(S_(t0), S_(ar), S_(bi), ALU.mult), [b_sm], [b_sm])
                v_(lambda e: e.tensor_tensor(S_(t1), S_(ai), S_(br), ALU.mult), [b_sm], [b_sm])
                v_(lambda e: e.tensor_tensor(S_(outi), S_(t0), S_(t1), ALU.add), [b_sm], [b_sm])

            v_(lambda e: e.tensor_tensor(S_("t0"), lr, lr, ALU.mult), [b_sp], [b_sm])
            v_(lambda e: e.tensor_tensor(S_("t1"), li, li, ALU.mult), [b_sp], [b_sm])
            v_(lambda e: e.tensor_tensor(S_("den"), S_("t0"), S_("t1"), ALU.add), [b_sm], [b_sm])
            v_(lambda e: e.reciprocal(S_("rden"), S_("den")), [b_sm], [b_sm])
            v_(lambda e: e.tensor_scalar(S_("t4"), S_("a_re"), -1.0, None, ALU.add), [b_sm], [b_sm])
            v_(lambda e: e.tensor_tensor(S_("t0"), S_("t4"), lr, ALU.mult), [b_sm, b_sp], [b_sm])
            v_(lambda e: e.tensor_tensor(S_("t1"), S_("a_im"), li, ALU.mult), [b_sm, b_sp], [b_sm])
            v_(lambda e: e.tensor_tensor(S_("t0"), S_("t0"), S_("t1"), ALU.add), [b_sm], [b_sm])
            v_(lambda e: e.tensor_tensor(S_("cr"), S_("t0"), S_("rden"), ALU.mult), [b_sm], [b_sm])
            v_(lambda e: e.tensor_tensor(S_("t0"), S_("a_im"), lr, ALU.mult), [b_sm, b_sp], [b_sm])
            v_(lambda e: e.tensor_tensor(S_("t1"), S_("t4"), li, ALU.mult), [b_sm, b_sp], [b_sm])
            v_(lambda e: e.tensor_tensor(S_("t0"), S_("t0"), S_("t1"), ALU.subtract), [b_sm], [b_sm])
            v_(lambda e: e.tensor_tensor(S_("ci"), S_("t0"), S_("rden"), ALU.mult), [b_sm], [b_sm])
            b32t = TMP[0][:, 0:1024].rearrange("p (r g c) -> p r g c", r=2, g=16)
            P.dma("sync", b32t, B32, reads=[], writes=[b_tmp[0]])
            cb3 = lambda n: S_(n).unsqueeze(2).to_broadcast([128, 16, 32])
            t5 = TMP[1][:, 0:512].rearrange("p (g c) -> p g c", c=32)
            t6 = TMP[1][:, 512:1024].rearrange("p (g c) -> p g c", c=32)
            v_(lambda e: e.tensor_tensor(t5, b32t[:, 0], cb3("cr"), ALU.mult), [b_tmp[0], b_sm], [b_tmp[1]])
            v_(lambda e: e.tensor_tensor(t6, b32t[:, 1], cb3("ci"), ALU.mult), [b_tmp[0], b_sm], [b_tmp[1]])
            v_(lambda e: e.tensor_tensor(bc[:, 0], t5, t6, ALU.subtract), [b_tmp[1]], [b_bc])
            v_(lambda e: e.tensor_tensor(t5, b32t[:, 1], cb3("cr"), ALU.mult), [b_tmp[0], b_sm, b_bc], [b_tmp[1]])
            v_(lambda e: e.tensor_tensor(t6, b32t[:, 0], cb3("ci"), ALU.mult), [b_tmp[0], b_sm], [b_tmp[1]])
            v_(lambda e: e.tensor_tensor(bc[:, 1], t5, t6, ALU.add), [b_tmp[1]], [b_bc])

            if True:
                assert NCH <= 32
                sloc = hTflat[:, 6144:6144 + 2 * 2 * NCH * 16].bitcast(F32).rearrange("p (r c g) -> p r c g", r=2, g=16)
                sin_ = qTflat[:, 0:2 * 2 * (NCH + 1) * 16].bitcast(F32).rearrange("p (r c g) -> p r c g", r=2, g=16)
            b_sloc = MB("sloc")
            b_sin = MB("sin")

            PK = [("a128r", "a128i"), ("dt", "rho"), ("th", "thr"), ("cr", "ci"), ("den", "rden"), ("t4", "t5")]
            NSTEP = int(math.ceil(math.log2(NCH + 1)))
            assert NSTEP <= len(PK)

            def chain_powers():
                for k in range(1, NSTEP):
                    cmul(PK[k][0], PK[k][1], PK[k - 1][0], PK[k - 1][1], PK[k - 1][0], PK[k - 1][1])

            def chain(first_from_zero):
                L1 = NCH + 1
                X0 = sin_
                X1 = TMP[3][:, 512:512 + 2 * L1 * 16].rearrange("p (r c g) -> p r c g", r=2, g=16)
                tA = xs2[:, 0:512]
                tB = TMP[3][:, 0:512]
                v_(lambda e: e.tensor_copy(sin_[:, :, 1:L1, :], sloc[:, :, 0:NCH, :]), [b_sloc, b_sin], [b_sin])
                bufs = [(X0, b_sin), (X1, b_tmp[3])]
                for k in range(NSTEP):
                    d = 1 << k
                    n = L1 - d
                    (Xo, bo), (Xn, bn) = bufs[k % 2], bufs[(k + 1) % 2]
                    pr_ = S_(PK[k][0]).unsqueeze(1).to_broadcast([128, n, 16])
                    pi_ = S_(PK[k][1]).unsqueeze(1).to_broadcast([128, n, 16])
                    ta = tA[:, 0:n * 16].rearrange("p (c g) -> p c g", g=16)
                    tb2 = tB[:, 0:n * 16].rearrange("p (c g) -> p c g", g=16)
                    rd = [bo, b_sm]
                    v_(lambda e, Xo=Xo, Xn=Xn, d=d: e.tensor_copy(Xn[:, :, 0:d, :], Xo[:, :, 0:d, :]), [bo], [bn])
                    v_(lambda e, Xo=Xo, n=n, pr_=pr_, ta=ta: e.tensor_tensor(ta, Xo[:, 0, 0:n, :], pr_, ALU.mult), rd, [b_fa])
                    v_(lambda e, Xo=Xo, n=n, pi_=pi_, tb2=tb2: e.tensor_tensor(tb2, Xo[:, 1, 0:n, :], pi_, ALU.mult), rd, [b_tmp[3]])
                    v_(lambda e, ta=ta, tb2=tb2: e.tensor_tensor(ta, ta, tb2, ALU.subtract), [b_fa, b_tmp[3]], [b_fa])
                    v_(lambda e, Xo=Xo, Xn=Xn, d=d, ta=ta: e.tensor_tensor(Xn[:, 0, d:L1, :], Xo[:, 0, d:L1, :], ta, ALU.add), [bo, b_fa], [bn])
                    v_(lambda e, Xo=Xo, n=n, pr_=pr_, ta=ta: e.tensor_tensor(ta, Xo[:, 1, 0:n, :], pr_, ALU.mult), rd, [b_fa])
                    v_(lambda e, Xo=Xo, n=n, pi_=pi_, tb2=tb2: e.tensor_tensor(tb2, Xo[:, 0, 0:n, :], pi_, ALU.mult), rd, [b_tmp[3]])
                    v_(lambda e, ta=ta, tb2=tb2: e.tensor_tensor(ta, ta, tb2, ALU.add), [b_fa, b_tmp[3]], [b_fa])
                    v_(lambda e, Xo=Xo, Xn=Xn, d=d, ta=ta: e.tensor_tensor(Xn[:, 1, d:L1, :], Xo[:, 1, d:L1, :], ta, ALU.add), [bo, b_fa], [bn])
                if NSTEP % 2 == 1:
                    v_(lambda e: e.tensor_copy(X0[:, :, :, :], X1[:, :, :, :]), [b_tmp[3]], [b_sin])

            xn = sb("xn", [128, D], BF16)
            b_xn = MB("xn")
            nrm = sb("nrm", [128, 8])
            b_nrm = MB("nrm")
            K.alt = 0

            def norm_block_to_hT(x_ap, b_x, gain_t, b_gain, hT, b_hT, col0, width=D):
                a_(lambda e: e.activation(xn[:], x_ap, AF.Square, accum_out=nrm[:, 0:1]), [b_x], [b_xn, b_nrm])
                a_(lambda e: e.activation(nrm[:, 1:2], nrm[:, 0:1], AF.Sqrt, bias=eps_c, scale=1.0 / width), [b_nrm, b_cn], [b_nrm])
                v_(lambda e: e.reciprocal(nrm[:, 2:3], nrm[:, 1:2]), [b_nrm], [b_nrm])
                a_(lambda e: e.activation(xn[:], x_ap, AF.Copy, scale=nrm[:, 2:3]), [b_x, b_nrm], [b_xn])
                for k4 in range(KC // 4):
                    tbank, tbankb = next_tbank()

                    def tr(e, k4=k4, tbank=tbank):
                        ins = None
                        for j in range(4):
                            kc = k4 * 4 + j
                            ins = e.transpose(tbank[:, j * 128:(j + 1) * 128], xn[:, kc * 128:(kc + 1) * 128], ident_b)
                        return ins
                    P.op("tensor", tr, reads=[b_xn, b_cb], writes=[tbankb])
                    for j in range(4):
                        kc = k4 * 4 + j
                        if (k4 % 2) == 0:
                            a_(lambda e, kc=kc, j=j, tbank=tbank: e.activation(hT[:, kc, col0:col0 + 128], tbank[:, j * 128:(j + 1) * 128],
                                                                               AF.Copy, scale=gain_t[:, kc:kc + 1]),
                               [tbankb, b_gain], [b_hT])
                        else:
                            v_(lambda e, kc=kc, j=j, tbank=tbank: e.tensor_scalar(hT[:, kc, col0:col0 + 128], tbank[:, j * 128:(j + 1) * 128],
                                                                                 gain_t[:, kc:kc + 1], None, ALU.mult),
                               [tbankb, b_gain], [b_hT])

            xa = [bigf[:, 0:2048], bigf[:, 2048:4096]]
            b_xa = [MB("xa%d" % i) for i in range(2)]
            hTa = bigf[:, 0:4096].bitcast(BF16).rearrange("p (k t) -> p k t", t=TT)
            utok4 = TMP[2].bitcast(BF16)[:, 0:NB * 512].rearrange("p (b c) -> p b c", c=512)
            b_hTa4 = [MB("hTa%d" % i) for i in range(NB)]
            b_utok4 = [MB("utok%d" % i) for i in range(NB)]
            a_bufs = b_hTa4 + b_utok4
            a_tmps = b_tmp[0:3]
            wsb = wp[:, 0:2, :].rearrange("p a n -> p (a n)").rearrange("p (k n) -> p k n", n=512)
            P.fence("vector", fence_fn, a_tmps, a_bufs)
            P.dma("sync", wsb, WINBv[:, :, 1536:2048], reads=[MB("winbs%d" % l)], writes=[b_wp[0], b_wp[1]])
            P.op("vector", lambda e: e.memset(sin_[:, :, 0, :], 0.0), reads=[], writes=[b_sin])
            seaf = sb("sea", [128, 8, 2, 16])
            smA = seaf[:].rearrange("p n r g -> p (n r g)")[:, 0:NB * 64].rearrange("p (b k g) -> p b k g", k=4, g=16)
            b_smA = [MB("sea")] * NB

            def a_stage2(c, tb):
                utok, b_utok = utok4[:, tb, :], b_utok4[tb]
                pb, pbb = next_bank()

                def mmu(e, pb=pb):
                    ins = None
                    for kc in range(KC):
                        ins = e.matmul(pb[:], lhsT=hTa[:, kc, tb * 128:(tb + 1) * 128], rhs=wsb[:, kc, :], start=(kc == 0), stop=(kc == KC - 1))
                    return ins
                P.op("tensor", mmu, reads=[b_hTa4[tb], b_wp[0], b_wp[1]], writes=[pbb])
                a_(lambda e, pb=pb: e.activation(utok, pb[:], AF.Copy), [pbb], [b_utok])
                pr, prb = next_bank()
                pi, pib = next_bank()

                def mmv(e, pr=pr, pi=pi):
                    ins = None
                    for gp in range(16):
                        e.matmul(pr[:, gp * 32:(gp + 1) * 32], lhsT=wm[:, 0, gp, :], rhs=utok[:, gp * 32:(gp + 1) * 32],
                                 start=True, stop=True)
                        ins = e.matmul(pi[:, gp * 32:(gp + 1) * 32], lhsT=wm[:, 1, gp, :], rhs=utok[:, gp * 32:(gp + 1) * 32],
                                       start=True, stop=True)
                    return ins
                P.op("tensor", mmv, reads=[b_wm, b_utok], writes=[prb, pib])
                f0, f1 = TMP[3][:, 0:512], TMP[3][:, 512:1024]
                bre, bim = bc[:, 0].rearrange("p g c -> p (g c)"), bc[:, 1].rearrange("p g c -> p (g c)")
                pnr, pni = smA[:, tb, 0, :], smA[:, tb, 1, :]
                bs_ = b_smA[tb]
                v_(lambda e, pr=pr: e.tensor_tensor(f0, bre, pr[:], ALU.mult), [b_bc, prb], [b_tmp[3]])
                v_(lambda e, pi=pi: e.tensor_tensor(f1, bim, pi[:], ALU.mult), [b_bc, pib], [b_tmp[3]])
                v_(lambda e: e.tensor_tensor(f0, f0, f1, ALU.subtract), [b_tmp[3]], [b_tmp[3]])
                v_(lambda e: e.tensor_reduce(pnr, f0.rearrange("p (g c) -> p g c", c=32), AX.X, ALU.add), [b_tmp[3]], [bs_])
                v_(lambda e, pi=pi: e.tensor_tensor(f0, bre, pi[:], ALU.mult), [b_bc, pib], [b_tmp[3]])
                v_(lambda e, pr=pr: e.tensor_tensor(f1, bim, pr[:], ALU.mult), [b_bc, prb], [b_tmp[3]])
                v_(lambda e: e.tensor_tensor(f0, f0, f1, ALU.add), [b_tmp[3]], [b_tmp[3]])
                v_(lambda e: e.tensor_reduce(pni, f0.rearrange("p (g c) -> p g c", c=32), AX.X, ALU.add), [b_tmp[3]], [bs_])
                t2, t3 = smA[:, tb, 2, :], smA[:, tb, 3, :]
                v_(lambda e: e.tensor_tensor(t2, S_("a127r"), pnr, ALU.mult), [b_sm, bs_], [bs_])
                v_(lambda e: e.tensor_tensor(t3, S_("a127i"), pni, ALU.mult), [b_sm, bs_], [bs_])
                v_(lambda e: e.tensor_tensor(sloc[:, 0, c, :], t2, t3, ALU.subtract), [bs_], [b_sloc])
                v_(lambda e: e.tensor_tensor(t2, S_("a127r"), pni, ALU.mult), [b_sm, bs_], [bs_])
                v_(lambda e: e.tensor_tensor(t3, S_("a127i"), pnr, ALU.mult), [b_sm, bs_], [bs_])
                v_(lambda e: e.tensor_tensor(sloc[:, 1, c, :], t2, t3, ALU.add), [bs_], [b_sloc])

            for g_ in range(NT):
                for tb in range(NB):
                    c = g_ * NB + tb
                    P.dma("sync", xt[:, tb, :], Xsrc[c * 128:(c + 1) * 128, :], reads=b_xsrc, writes=[b_xt[tb]])
                for tb in range(NB):
                    norm_block_to_hT(xt[:, tb, :], b_xt[tb], gmix, b_gmix, hTa, b_hTa4[tb], tb * 128)
                for tb in range(NB):
                    a_stage2(g_ * NB + tb, tb)
            chain_powers()
            chain(True)
            send_t = sb("send_t", [128, 32])
            b_send = MB("send")
            v_(lambda e: e.tensor_copy(send_t[:, 0:16], sin_[:, 0, NCH, :]), [b_sin], [b_send])
            v_(lambda e: e.tensor_copy(send_t[:, 16:32], sin_[:, 1, NCH, :]), [b_sin], [b_send])
            b_sendb, b_sendg = MB("sendb%d" % l), MB("sendg%d" % l)
            P.dma("sync", SENDB[l], send_t[:], reads=[b_send], writes=[b_sendb])
            P.coll(lambda e: e.collective_compute("AllGather", ALU.bypass, replica_groups=[list(range(NCORES))],
                                                  ins=[SENDB[l].opt()], outs=[SENDG[l].opt()]),
                   reads=[b_sendb], writes=[b_sendg])
            P.fence("vector", fence_fn, a_bufs, a_tmps)
            if l + 1 < depth:
                emit_casts(l + 1)

            ctb = sb("ctb", [128, 2, 16, 128], BF16)
            b_ctb = MB("ctb")
            ctf = TMP[0][:, 0:2048].rearrange("p (g c) -> p g c", c=128)
            for ri in range(2):
                P.dma("sync", ctf, CTP[:, ri], reads=[], writes=[b_tmp[0]])
                if ri == 0:
                    v_(lambda e: e.tensor_copy(ctb[:, 0], ctf), [b_tmp[0]], [b_ctb])
                else:
                    v_(lambda e: e.tensor_scalar(ctb[:, 1], ctf, -1.0, None, ALU.mult), [b_tmp[0]], [b_ctb])
            btb = sb("btb", [128, 2, 16, 128], BF16)
            b_btb = MB("btb")
            bsrc = TMP[3][:].bitcast(BF16)[:, 0:4096].rearrange("p (r g c) -> p r g c", r=2, g=16)
            P.op("vector", lambda e: e.memset(bsrc, 0.0), reads=[], writes=[b_tmp[3]])
            for ri in range(2):
                for j in range(4):
                    src = bc[:, ri].rearrange("p (i j) c -> p i j c", j=4)[:, :, j, :]
                    dst = bsrc[:, ri].rearrange("p (i j) c -> p i j c", j=4)[:, :, j, 32 * j:32 * j + 32]
                    P.op("vector", lambda e, src=src, dst=dst: e.tensor_copy(dst, src), reads=[b_bc, b_tmp[3]], writes=[b_tmp[3]])
            for ri in range(2):
                for g4 in range(4):
                    tbank, tbankb = next_tbank()

                    def tr(e, ri=ri, g4=g4, tbank=tbank):
                        ins = None
                        for j in range(4):
                            ins = e.transpose(tbank[:, j * 128:(j + 1) * 128], bsrc[:, ri, g4 * 4 + j, :], ident_b)
                        return ins
                    P.op("tensor", tr, reads=[b_tmp[3], b_cb], writes=[tbankb])
                    v_(lambda e, ri=ri, g4=g4, tbank=tbank: e.tensor_copy(btb[:, ri, g4 * 4:(g4 + 1) * 4, :],
                                                                          tbank[:, 0:512].rearrange("p (j c) -> p j c", c=128)),
                       [tbankb], [b_btb])
            wglu = sb("wglu", [128, 4, 512], BF16)
            b_wglu = MB("wglu")
            wgf = TMP[4][:, 0:2048].rearrange("p (i c) -> p i c", c=512)
            P.dma("sync", wgf, WGLU.rearrange("(i p) c -> p i c", p=128), reads=[], writes=[b_tmp[4]])
            v_(lambda e: e.tensor_copy(wglu[:], wgf), [b_tmp[4]], [b_wglu])
            wst = sb("wst", [128, 4, 128], BF16)
            b_wst = MB("wst")
            wsf2 = TMP[1][:, 0:512].rearrange("p (g s) -> p g s", s=128)
            P.dma("sync", wsf2, WS, reads=[], writes=[b_tmp[1]])
            for g in range(4):
                pb, pbb = next_bank()
                P.op("tensor", lambda e, pb=pb, g=g: e.transpose(pb[:, 0:128], wsf2[:, g, :], ident_f), reads=[b_tmp[1], b_cn], writes=[pbb])
                v_(lambda e, pb=pb, g=g: e.tensor_tensor(wst[:, g, :], pb[:, 0:128], tri_f, ALU.mult), [pbb, b_cn], [b_wst])
            bsb = sb("bsb", [128, 512])
            lngb = sb("lngb", [128, 1024])
            esink = sb("esink", [128, 16])
            gatt = sb("gatt", [128, 8])
            gffn = sb("gffn", [128, KC])
            b_misc = MB("misc")
            P.dma("sync", bsb[:], BS.partition_broadcast(128), writes=[b_misc])
            P.dma("sync", lngb[:], LNGB.partition_broadcast(128), writes=[b_misc])
            P.dma("sync", esink[:], SINK.partition_broadcast(128), writes=[b_misc])
            P.dma("sync", gatt[:], GATT, writes=[b_misc])
            P.dma("sync", gffn[:], GFFN, writes=[b_misc])
            a_(lambda e: e.activation(esink[:], esink[:], AF.Exp), [b_misc], [b_misc])
            dsk = spt[:, 48:52]
            gssm = spt[:, 52:56]
            ggm = spt[:, 56:60]

            sea = sb("sea", [128, 8, 2, 16])[:, 0:NCORES]
            b_sea = MB("sea")
            P.dma("sync", sea[:].rearrange("p n r g -> p n (r g)"), SENDG[l].rearrange("(n p) c -> p n c", p=128),
                  reads=[MB("sendg%d" % l)], writes=[b_sea])
            v_(lambda e: e.tensor_copy(S_("pnr"), S_("a128r")), [b_sm], [b_sm])
            v_(lambda e: e.tensor_copy(S_("pni"), S_("a128i")), [b_sm], [b_sm])
            nsq = int(round(math.log2(NCH)))
            assert (1 << nsq) == NCH
            for _ in range(nsq):
                cmul("p2r", "p2i", "pnr", "pni", "pnr", "pni")
                v_(lambda e: e.tensor_copy(S_("pnr"), S_("p2r")), [b_sm], [b_sm])
                v_(lambda e: e.tensor_copy(S_("pni"), S_("p2i")), [b_sm], [b_sm])
            cmul("p2r", "p2i", "pnr", "pni", "pnr", "pni")
            tsel = sb("tsel", [128, 3, 2, 16])
            b_tsel = MB("tsel")
            P.op("vector", lambda e: e.memset(tsel[:], 0.0), reads=[], writes=[b_tsel])
            for i in range(NCORES):
                for m in range(3):
                    for ri in range(2):
                        v_(lambda e, i=i, m=m, ri=ri: e.scalar_tensor_tensor(
                            tsel[:, m, ri, :], sea[:, i, ri, :], cn[:, C_SEL + i * 3 + m:C_SEL + i * 3 + m + 1], tsel[:, m, ri, :],
                            ALU.mult, ALU.add), [b_sea, b_cn, b_tsel], [b_tsel])
            for m, (pr_, pi_) in ((1, ("pnr", "pni")), (2, ("p2r", "p2i"))):
                v_(lambda e, m=m, pr_=pr_: e.tensor_tensor(S_("t2"), S_(pr_), tsel[:, m, 0, :], ALU.mult), [b_sm, b_tsel], [b_sm])
                v_(lambda e, m=m, pi_=pi_: e.tensor_tensor(S_("t3"), S_(pi_), tsel[:, m, 1, :], ALU.mult), [b_sm, b_tsel], [b_sm])
                v_(lambda e: e.tensor_tensor(S_("t2"), S_("t2"), S_("t3"), ALU.subtract), [b_sm], [b_sm])
                v_(lambda e: e.tensor_tensor(tsel[:, 0, 0, :], tsel[:, 0, 0, :], S_("t2"), ALU.add), [b_sm, b_tsel], [b_tsel])
                v_(lambda e, m=m, pr_=pr_: e.tensor_tensor(S_("t2"), S_(pr_), tsel[:, m, 1, :], ALU.mult), [b_sm, b_tsel], [b_sm])
                v_(lambda e, m=m, pi_=pi_: e.tensor_tensor(S_("t3"), S_(pi_), tsel[:, m, 0, :], ALU.mult), [b_sm, b_tsel], [b_sm])
                v_(lambda e: e.tensor_tensor(S_("t2"), S_("t2"), S_("t3"), ALU.add), [b_sm], [b_sm])
                v_(lambda e: e.tensor_tensor(tsel[:, 0, 1, :], tsel[:, 0, 1, :], S_("t2"), ALU.add), [b_sm, b_tsel], [b_tsel])
            v_(lambda e: e.tensor_copy(sin_[:, 0, 0, :], tsel[:, 0, 0, :]), [b_tsel], [b_sin])
            v_(lambda e: e.tensor_copy(sin_[:, 1, 0, :], tsel[:, 0, 1, :]), [b_tsel], [b_sin])
            chain(False)
            asin = sb("asin", [128, 2, NCH, 16])
            b_asin = MB("asin")
            abr = lambda n: S_(n).unsqueeze(1).to_broadcast([128, NCH, 16])
            ta = TMP[1][:, 0:NCH * 16].rearrange("p (c g) -> p c g", g=16)
            tb_ = TMP[1][:, 512:512 + NCH * 16].rearrange("p (c g) -> p c g", g=16)
            v_(lambda e: e.tensor_tensor(ta, sin_[:, 0, 0:NCH, :], abr("a_re"), ALU.mult), [b_sin, b_sm, b_wst], [b_tmp[1]])
            v_(lambda e: e.tensor_tensor(tb_, sin_[:, 1, 0:NCH, :], abr("a_im"), ALU.mult), [b_sin, b_sm], [b_tmp[1]])
            v_(lambda e: e.tensor_tensor(asin[:, 0], ta, tb_, ALU.subtract), [b_tmp[1]], [b_asin])
            v_(lambda e: e.tensor_tensor(ta, sin_[:, 0, 0:NCH, :], abr("a_im"), ALU.mult), [b_sin, b_sm, b_asin], [b_tmp[1]])
            v_(lambda e: e.tensor_tensor(tb_, sin_[:, 1, 0:NCH, :], abr("a_re"), ALU.mult), [b_sin, b_sm], [b_tmp[1]])
            v_(lambda e: e.tensor_tensor(asin[:, 1], ta, tb_, ALU.add), [b_tmp[1]], [b_asin])


            K.wjobs = []
            K.wnext = 0
            K.wslot = 0

            def wjob(kind, src_ap, reads):
                K.wjobs.append(dict(kind=kind, src=src_ap, reads=reads, slot=None))
                return len(K.wjobs) - 1

            def wissue_upto(j):
                while K.wnext <= j and K.wnext < len(K.wjobs):
                    jb = K.wjobs[K.wnext]
                    if jb["kind"] == "big":
                        if K.wslot % 2:
                            K.wslot += 1
                        s = K.wslot % 4
                        K.wslot += 2
                        dst = wp[:, s:s + 2, :].rearrange("p a n -> p (a n)")
                        bufs = [b_wp[s], b_wp[s + 1]]
                    else:
                        s = K.wslot % 4
                        K.wslot += 1
                        dst = wp[:, s, :]
                        bufs = [b_wp[s]]
                    src = jb["src"]
                    dshape = src.shape
                    if len(dshape) == 3:
                        dstv = dst[:, 0:dshape[1] * dshape[2]].rearrange("p (a n) -> p a n", n=dshape[2])
                    else:
                        dstv = dst[:, 0:dshape[1]]
                    P.dma("sync", dstv, src, reads=jb["reads"], writes=bufs)
                    jb["slot"] = (dstv, bufs)
                    K.wnext += 1

            def wget(j):
                wissue_upto(j + 1)
                return K.wjobs[j]["slot"]

            WINBv = WINB.rearrange("(kc p) n -> p kc n", p=128)
            WOUTBv = WOUTB.rearrange("(kc p) n -> p kc n", p=128)
            WGBv = WGB.rearrange("(kc p) n -> p kc n", p=128)
            WUBv = WUB.rearrange("(kc p) n -> p kc n", p=128)
            WDBv = WDB.rearrange("(hb p) n -> p hb n", p=128)
            HPS = 2
            NST = NHB // HPS
            DPC = 16
            pieces = [(a, min(NHB, a + DPC)) for a in range(0, NHB, DPC)]
            hj = wjob("big", WINBv[:, :, 1024:1536], [b_winb])
            tile_jobs = []
            for T in range(NT):
                J = {}
                J["in"] = {i: wjob("big", WINBv[:, :, i * 512:(i + 1) * 512], [b_winb if i < 3 else MB("winbs%d" % l)]) for i in (3, 4, 5, 0, 1, 2)}
                J["out"] = [wjob("big", WOUTBv[:, :, i * 512:(i + 1) * 512], [b_woutb]) for i in range(4)]
                J["gu"] = []
                for s_ in range(NST):
                    J["gu"].append((wjob("half", WGBv[:, :, s_ * 256:(s_ + 1) * 256], [MB("wgb%d_%d" % (l, s_))]),
                                    wjob("half", WUBv[:, :, s_ * 256:(s_ + 1) * 256], [MB("wub%d_%d" % (l, s_))])))
                J["dn"] = [[wjob("big", WDBv[:, a:b, cb * 512:(cb + 1) * 512], [MB("wdb%d_%d" % (l, pi_))]) for pi_, (a, b) in enumerate(pieces)] for cb in range(4)]
                tile_jobs.append(J)

            def evac(i, fn_act, fn_vec, reads, writes):
                if i % 2 == 0:
                    a_(fn_act, reads, writes)
                else:
                    v_(fn_vec, reads, writes)

            def rms_feature_major(yp, b_yp, sqf, rs, b_scr, nchunk, width, gain_ap, dst_chunk0, tok0):
                sq = sqf[:, 0:nchunk * 64].bitcast(BF16).rearrange("p (c t) -> p c t", t=128)
                a_(lambda e: e.activation(sq, yp[:, 0:nchunk, :], AF.Square), [b_yp], b_scr)
                pb, pbb = next_bank()

                def mmss(e, pb=pb):
                    ins = None
                    for c in range(nchunk):
                        ins = e.matmul(pb[:, 0:128], lhsT=ones_b, rhs=sq[:, c, :], start=(c == 0), stop=(c == nchunk - 1))
                    return ins
                P.op("tensor", mmss, reads=b_scr + [b_cb], writes=[pbb])
                a_(lambda e, pb=pb: e.activation(rs, pb[:, 0:128], AF.Sqrt, bias=eps_c, scale=1.0 / width), [pbb, b_cn], b_scr)
                v_(lambda e: e.reciprocal(rs, rs), b_scr, b_scr)
                for c in range(nchunk):
                    P.op("vector", lambda e, c=c: e.scalar_tensor_tensor(ynT[:, dst_chunk0 + c, tok0:tok0 + 128], yp[:, c, :],
                                                                      gain_ap[:, c:c + 1], rs, ALU.mult, ALU.mult),
                         reads=[b_yp, b_misc, b_sp] + b_scr, writes=[b_ynT])

            xh = TMP[0][:, 0:D]
            if l == 0:
                P.dma("sync", xh, XH, reads=[], writes=[b_tmp[0]])
            else:
                stg = TMP[1][:, 0:D]
                v_(lambda e: e.memset(xh, 0.0), [], [b_tmp[0]])
                for i in range(NCORES):
                    P.dma("sync", stg, XHG[i * 128:(i + 1) * 128, :], reads=[MB("xhg")], writes=[b_tmp[1]])
                    v_(lambda e, i=i: e.scalar_tensor_tensor(xh, stg, cn[:, C_SELH + i:C_SELH + i + 1], xh, ALU.mult, ALU.add),
                       [b_tmp[1], b_tmp[0], b_cn], [b_tmp[0]])
            hTh = TMP[4][:].bitcast(BF16)[:, 0:KC * 128].rearrange("p (k t) -> p k t", t=128)
            norm_block_to_hT(xh, b_tmp[0], gmix, b_gmix, hTh, b_tmp[4], 0)

            def emit_kv(hsrc, b_hsrc, wv, bufs_w, tokcols, slot0, nblk):
                ntok = nblk * 128
                for kv in range(2):
                    pb, pbb = next_bank()

                    def mmk(e, pb=pb, kv=kv):
                        ins = None
                        for kc in range(KC):
                            ins = e.matmul(pb[:, 0:ntok], lhsT=wv[:, kc, kv * 128:(kv + 1) * 128], rhs=hsrc[:, kc, tokcols],
                                           start=(kc == 0), stop=(kc == KC - 1))
                        return ins
                    P.op("tensor", mmk, reads=[b_hsrc] + bufs_w, writes=[pbb])
                    a_(lambda e, pb=pb, kv=kv: e.activation(kT[0:64, kv, 0, slot0:slot0 + nblk, :],
                                                            pb[0:64, 0:ntok].rearrange("p (b t) -> p b t", t=128), AF.Copy),
                       [pbb], [b_kT])
                    v_(lambda e, pb=pb, kv=kv: e.tensor_copy(kT[64:128, kv, 1, slot0:slot0 + nblk, :],
                                                             pb[64:128, 0:ntok].rearrange("p (b t) -> p b t", t=128)),
                       [pbb], [b_kT])
                for tb in range(nblk):
                    pb, pbb = next_bank()
                    t0_ = tokcols.start + tb * 128

                    def mmv2(e, pb=pb, t0_=t0_):
                        ins = None
                        for kc in range(KC):
                            ins = e.matmul(pb[:, 0:128], lhsT=hsrc[:, kc, t0_:t0_ + 128], rhs=wv[:, kc, 256:384],
                                           start=(kc == 0), stop=(kc == KC - 1))
                        return ins
                    P.op("tensor", mmv2, reads=[b_hsrc] + bufs_w, writes=[pbb])
                    src = pb[:, 0:128].rearrange("p (k d) -> p k d", d=64)
                    for dup in range(2):
                        dst = vtok[:, slot0 + tb, :, dup * 64:(dup + 1) * 64]
                        evac(dup, lambda e, src=src, dst=dst: e.activation(dst, src, AF.Copy),
                             lambda e, src=src, dst=dst: e.tensor_copy(dst, src), [pbb], [b_vtok])

            wv, bw_ = wget(hj)
            emit_kv(hTh, b_tmp[4], wv, bw_, slice(0, 128), 0, 1)

            P.fence("vector", fence_fn, b_tmp + [b_wmt, b_bc, b_sloc, b_sin], [b_hT, b_qT] + mixer_bufs + b_act)
            for T in range(NT):
                J = tile_jobs[T]
                tok_base = T * TT
                for tb in range(NB):
                    P.dma("sync", xt[:, tb, :], Xsrc[tok_base + tb * 128:tok_base + (tb + 1) * 128, :], reads=b_xsrc, writes=[b_xt[tb]])
                P.fence("vector", fence_fn, b_E, [b_hT])
                for tb in range(NB):
                    norm_block_to_hT(xt[:, tb, :], b_xt[tb], gmix, b_gmix, hT, b_hT, tb * 128)
                P.fence("vector", fence_fn, b_act, mixer_bufs)
                wv, bw_ = wget(J["in"][3])
                for j in range(4):
                    pb, pbb = next_bank()

                    def mmu2(e, pb=pb, wv=wv, j=j):
                        ins = None
                        for kc in range(KC):
                            ins = e.matmul(pb[:, 0:TT], lhsT=wv[:, kc, j * 128:(j + 1) * 128], rhs=hT[:, kc, :],
                                           start=(kc == 0), stop=(kc == KC - 1))
                        return ins
                    P.op("tensor", mmu2, reads=[b_hT] + bw_, writes=[pbb])
                    evac(j, lambda e, pb=pb, j=j: e.activation(uT[:, j, :], pb[:, 0:TT], AF.Copy),
                         lambda e, pb=pb, j=j: e.tensor_copy(uT[:, j, :], pb[:, 0:TT]), [pbb], [b_uT])
                wv, bw_ = wget(J["in"][4])
                for j in range(4):
                    pb, pbb = next_bank()

                    def mmzu(e, pb=pb, wv=wv, j=j):
                        ins = None
                        for kc in range(KC):
                            ins = e.matmul(pb[:, 0:TT], lhsT=wv[:, kc, j * 128:(j + 1) * 128], rhs=hT[:, kc, :],
                                           start=(kc == 0), stop=(kc == KC - 1))
                        return ins
                    P.op("tensor", mmzu, reads=[b_hT] + bw_, writes=[pbb])
                    a_(lambda e, pb=pb, j=j: e.activation(guT[:, j, :], pb[:, 0:TT], AF.Gelu), [pbb], [b_guT])
                wv, bw_ = wget(J["in"][5])
                for tb in range(NB):
                    pb, pbb = next_bank()

                    def mmzv(e, pb=pb, wv=wv, tb=tb):
                        ins = None
                        for kc in range(KC):
                            ins = e.matmul(pb[:], lhsT=hT[:, kc, tb * 128:(tb + 1) * 128], rhs=wv[:, kc, :],
                                           start=(kc == 0), stop=(kc == KC - 1))
                        return ins
                    P.op("tensor", mmzv, reads=[b_hT] + bw_, writes=[pbb])
                    gv = ft0[:, 0:512]
                    a_(lambda e, pb=pb: e.activation(gv, pb[:], AF.Gelu, accum_out=st8[:, 0:1]), [pbb], [b_ft[0], b_st8])
                    a_(lambda e: e.activation(ft1[:, 0:512], gv, AF.Square, accum_out=st8[:, 1:2]), [b_ft[0]], [b_ft[1], b_st8])
                    v_(lambda e: e.tensor_scalar(st8[:, 2:3], st8[:, 0:1], 1.0 / 512, None, ALU.mult), [b_st8], [b_st8])
                    v_(lambda e: e.tensor_tensor(st8[:, 3:4], st8[:, 2:3], st8[:, 2:3], ALU.mult), [b_st8], [b_st8])
                    v_(lambda e: e.scalar_tensor_tensor(st8[:, 4:5], st8[:, 1:2], 1.0 / 512, st8[:, 3:4], ALU.mult, ALU.subtract),
                       [b_st8], [b_st8])
                    a_(lambda e: e.activation(st8[:, 5:6], st8[:, 4:5], AF.Sqrt, bias=eps_c), [b_st8, b_cn], [b_st8])
                    v_(lambda e: e.reciprocal(st8[:, 6:7], st8[:, 5:6]), [b_st8], [b_st8])
                    v_(lambda e: e.tensor_scalar(gv, gv, st8[:, 2:3], st8[:, 6:7], ALU.subtract, ALU.mult), [b_ft[0], b_st8], [b_ft[0]])
                    P.op("vector", lambda e: e.tensor_tensor(gv, gv, lngb[:, 0:512], ALU.mult), reads=[b_ft[0], b_misc], writes=[b_ft[0]])
                    P.op("vector", lambda e, tb=tb: e.tensor_tensor(vg[:, tb, :], gv, lngb[:, 512:1024], ALU.add),
                         reads=[b_ft[0], b_misc], writes=[b_vg])

                def inproj_qkv():
                    for i in range(2):
                        wv, bw_ = wget(J["in"][i])
                        for j in range(4):
                            cbk = i * 4 + j
                            pb, pbb = next_bank()

                            def mmq(e, pb=pb, wv=wv, j=j):
                                ins = None
                                for kc in range(KC):
                                    ins = e.matmul(pb[:, 0:TT], lhsT=wv[:, kc, j * 128:(j + 1) * 128], rhs=hT[:, kc, :],
                                                   start=(kc == 0), stop=(kc == KC - 1))
                                return ins
                            P.op("tensor", mmq, reads=[b_hT] + bw_, writes=[pbb])
                            evac(cbk, lambda e, pb=pb, cbk=cbk: e.activation(qT[:, cbk, :], pb[:, 0:TT], AF.Copy),
                                 lambda e, pb=pb, cbk=cbk: e.tensor_copy(qT[:, cbk, :], pb[:, 0:TT]), [pbb], [b_qT])
                    wv, bw_ = wget(J["in"][2])
                    emit_kv(hT, b_hT, wv, bw_, slice(0, TT), 1, NB)


                def emit_scores(ks):
                    Eb, bE = EB[ks % 2], b_E[ks % 2]
                    if ks == 0:
                        q0, N, mk, c0 = 0, 128, (mh_b if T == 0 else mcp_b[:, 128:256]), 128
                    elif ks == NB:
                        q0, N, mk, c0 = (NB - 1) * 128, 128, mcp_b[:, 0:128], 0
                    else:
                        q0, N, mk, c0 = (ks - 1) * 128, 256, mcp_b[:, 0:256], 0
                    for hp in range(8):
                        pb, pbb = next_bank()
                        kv = hp // 4

                        def mms(e, pb=pb, hp=hp, kv=kv):
                            ins = None
                            for hh in range(2):
                                e.matmul(pb[:, hh * 256:hh * 256 + N], lhsT=kT[:, kv, hh, ks, :], rhs=qT[:, hp, q0:q0 + N],
                                         start=True, stop=False)
                                ins = e.matmul(pb[:, hh * 256:hh * 256 + N], lhsT=ident_b, rhs=mk, start=False, stop=True)
                            return ins
                        P.op("tensor", mms, reads=[b_kT, b_qT, b_cb], writes=[pbb])
                        a_(lambda e, pb=pb, hp=hp: e.activation(Eb[:, 2 * hp:2 * hp + 2, c0:c0 + N],
                                                                pb[:].rearrange("p (h q) -> p h q", q=256)[:, :, 0:N],
                                                                AF.Exp, scale=HD ** -0.5), [pbb], [bE])

                def emit_pv(n, c_lo, c_hi, do_rms):
                    E0, bE0 = EB[n % 2], b_E[n % 2]
                    E1, bE1 = EB[(n + 1) % 2], b_E[(n + 1) % 2]
                    for c in range(c_lo, c_hi):
                        kv = c // 4
                        pb, pbb = next_bank()

                        def mmpv(e, pb=pb, c=c, kv=kv):
                            ins = None
                            for hh in range(2):
                                h = 2 * c + hh
                                e.matmul(pb[:, hh * 128:(hh + 1) * 128], lhsT=vtok[:, n, kv, :], rhs=E0[:, h, 128:256], start=True, stop=False)
                                e.matmul(pb[:, hh * 128:(hh + 1) * 128], lhsT=vtok[:, n + 1, kv, :], rhs=E1[:, h, 0:128], start=False, stop=True)
                                e.matmul(pb[:, 256 + hh * 128:256 + (hh + 1) * 128], lhsT=ones_b, rhs=E0[:, h, 128:256], start=True, stop=False)
                                ins = e.matmul(pb[:, 256 + hh * 128:256 + (hh + 1) * 128], lhsT=ones_b, rhs=E1[:, h, 0:128], start=False, stop=True)
                            return ins
                        P.op("tensor", mmpv, reads=[b_vtok, bE0, bE1, b_cb], writes=[pbb])
                        for hh in range(2):
                            h = 2 * c + hh
                            a_(lambda e, pb=pb, hh=hh, h=h: e.activation(rr_a[:, hh * 128:(hh + 1) * 128], pb[:, 256 + hh * 128:256 + (hh + 1) * 128],
                                                                         AF.Identity, bias=esink[:, h:h + 1]), [pbb, b_misc], [b_fa])
                        v_(lambda e: e.reciprocal(rr_a, rr_a), [b_fa], [b_fa])
                        v_(lambda e, pb=pb, c=c: e.tensor_tensor(ypre_a[0:64, c, :], pb[0:64, 0:128], rr_a[0:64, 0:128], ALU.mult),
                           [pbb, b_fa], [b_ya])
                        v_(lambda e, pb=pb, c=c: e.tensor_tensor(ypre_a[64:128, c, :], pb[64:128, 128:256], rr_a[64:128, 128:256], ALU.mult),
                           [pbb, b_fa], [b_ya])
                    if do_rms:
                        rms_feature_major(ypre_a, b_ya, sq_a, rs_a, [b_fa], 8, 1024, gatt, 0, n * 128)

                def emit_gmlp(tb):
                    for g in range(4):
                        pb, pbb = next_bank()
                        P.op("tensor", lambda e, pb=pb, g=g: e.matmul(pb[:, 0:128], lhsT=vg[:, tb, g * 128:(g + 1) * 128], rhs=wst[:, g, :],
                                                                    start=True, stop=True), reads=[b_vg, b_wst], writes=[pbb])
                        v_(lambda e, pb=pb, g=g: e.tensor_tensor(tmpm_g, pb[:, 0:128], bsb[:, g * 128:(g + 1) * 128], ALU.add), [pbb, b_misc], [b_fg])
                        v_(lambda e, g=g: e.tensor_tensor(ypre_g[:, g, :], tmpm_g, guT[:, g, tb * 128:(tb + 1) * 128], ALU.mult),
                           [b_fg, b_guT], [b_yg])
                    rms_feature_major(ypre_g, b_yg, sq_g, rs_g, [b_fg], 4, 512, ggm, 12, tb * 128)

                K.ssm = {}

                def ssm_A(tb, half):
                    cg = T * NB + tb
                    ts_ = slice(tb * 128, (tb + 1) * 128)
                    pre = [next_bank() for _ in range(2)]
                    pim = [next_bank() for _ in range(2)]

                    def mmbu(e, pre=pre, pim=pim, half=half, ts_=ts_):
                        ins = None
                        for gl in range(8):
                            gp = half * 8 + gl
                            i = gp // 4
                            e.matmul(pre[gl // 4][0][:, (gl % 4) * 128:(gl % 4 + 1) * 128], lhsT=uT[:, i, ts_], rhs=btb[:, 0, gp, :],
                                     start=True, stop=True)
                            ins = e.matmul(pim[gl // 4][0][:, (gl % 4) * 128:(gl % 4 + 1) * 128], lhsT=uT[:, i, ts_], rhs=btb[:, 1, gp, :],
                                           start=True, stop=True)
                        return ins
                    P.op("tensor", mmbu, reads=[b_uT, b_btb], writes=[pre[0][1], pre[1][1], pim[0][1], pim[1][1]])
                    for q in range(2):
                        gs = slice(half * 8 + q * 4, half * 8 + q * 4 + 4)
                        wre = wm[:, 0, gs, :].rearrange("p g c -> p (g c)")
                        wim = wm[:, 1, gs, :].rearrange("p g c -> p (g c)")
                        pr_, prb_ = pre[q]
                        pi_, pib_ = pim[q]
                        f0 = ft0[:, 0:512]
                        f1 = ft1[:, 0:512]
                        zs = slice(q * 512, (q + 1) * 512)
                        v_(lambda e, pr_=pr_, wre=wre: e.tensor_tensor(f0, pr_[:], wre, ALU.mult), [prb_, b_wm], [b_ft[0]])
                        v_(lambda e, pi_=pi_, wim=wim: e.tensor_tensor(f1, pi_[:], wim, ALU.mult), [pib_, b_wm], [b_ft[1]])
                        P.op("vector", lambda e, zs=zs: e.tensor_tensor(zb[:, 0, zs], f0, f1, ALU.subtract), reads=b_ft, writes=[b_zb])
                        v_(lambda e, pr_=pr_, wim=wim: e.tensor_tensor(f0, pr_[:], wim, ALU.mult), [prb_, b_wm], [b_ft[0]])
                        v_(lambda e, pi_=pi_, wre=wre: e.tensor_tensor(f1, pi_[:], wre, ALU.mult), [pib_, b_wm], [b_ft[1]])
                        P.op("vector", lambda e, zs=zs: e.tensor_tensor(zb[:, 1, zs], f0, f1, ALU.add), reads=b_ft, writes=[b_zb])
                    for gl in range(8):
                        gp = half * 8 + gl
                        for ri in range(2):
                            a_(lambda e, gl=gl, ri=ri, gp=gp: e.activation(sbb[:, ri, gl, :], ident_b, AF.Copy, scale=asin[:, ri, cg, gp:gp + 1]),
                               [b_cb, b_asin], [b_sbb])
                    K.ssm[(tb, half)] = (pre, pim)

                def ssm_B(tb, half):
                    cg = T * NB + tb
                    ts_ = slice(tb * 128, (tb + 1) * 128)
                    pre, pim = K.ssm[(tb, half)]
                    rre = [next_bank() for _ in range(2)]
                    rim = [next_bank() for _ in range(2)]

                    def mmr(e, rre=rre, rim=rim):
                        ins = None
                        for gl in range(8):
                            cs = slice((gl % 4) * 128, (gl % 4 + 1) * 128)
                            e.matmul(rre[gl // 4][0][:, cs], lhsT=zb[:, 0, gl * 128:(gl + 1) * 128], rhs=tri_b, start=True, stop=False)
                            e.matmul(rre[gl // 4][0][:, cs], lhsT=sbb[:, 0, gl, :], rhs=ones_b, start=False, stop=True)
                            e.matmul(rim[gl // 4][0][:, cs], lhsT=zb[:, 1, gl * 128:(gl + 1) * 128], rhs=tri_b, start=True, stop=False)
                            ins = e.matmul(rim[gl // 4][0][:, cs], lhsT=sbb[:, 1, gl, :], rhs=ones_b, start=False, stop=True)
                        return ins
                    P.op("tensor", mmr, reads=[b_zb, b_sbb, b_cb], writes=[rre[0][1], rre[1][1], rim[0][1], rim[1][1]])
                    for q in range(2):
                        gs = slice(half * 8 + q * 4, half * 8 + q * 4 + 4)
                        rr_, rrb_ = rre[q]
                        ri_, rib_ = rim[q]
                        rv = rr_[:].rearrange("p (g t) -> p g t", t=128)
                        iv = ri_[:].rearrange("p (g t) -> p g t", t=128)
                        wpr = wpt[:, 0, gs, 0:128]
                        wpi = wpt[:, 1, gs, 0:128]
                        m0 = ft0[:, 0:512].rearrange("p (g t) -> p g t", t=128)
                        m1 = ft1[:, 0:512].rearrange("p (g t) -> p g t", t=128)
                        so_r = sbb[:, 0, q * 4:(q + 1) * 4, :]
                        so_i = sbb[:, 1, q * 4:(q + 1) * 4, :]
                        v_(lambda e, rv=rv, wpr=wpr, m0=m0: e.tensor_tensor(m0, rv, wpr, ALU.mult), [rrb_, b_wpt], [b_ft[0]])
                        v_(lambda e, iv=iv, wpi=wpi, m1=m1: e.tensor_tensor(m1, iv, wpi, ALU.mult), [rib_, b_wpt], [b_ft[1]])
                        v_(lambda e, so_r=so_r, m0=m0, m1=m1: e.tensor_tensor(so_r, m0, m1, ALU.subtract), b_ft, [b_sbb])
                        v_(lambda e, rv=rv, wpi=wpi, m0=m0: e.tensor_tensor(m0, rv, wpi, ALU.mult), [rrb_, b_wpt], [b_ft[0]])
                        v_(lambda e, iv=iv, wpr=wpr, m1=m1: e.tensor_tensor(m1, iv, wpr, ALU.mult), [rib_, b_wpt], [b_ft[1]])
                        v_(lambda e, so_i=so_i, m0=m0, m1=m1: e.tensor_tensor(so_i, m0, m1, ALU.add), b_ft, [b_sbb])
                    for ii in range(2):
                        i = half * 2 + ii
                        pb, pbb = next_bank()

                        def mmy(e, pb=pb, ii=ii, i=i, half=half):
                            ins = None
                            for j in range(4):
                                gl = ii * 4 + j
                                gp = half * 8 + gl
                                e.matmul(pb[:, 0:128], lhsT=ctb[:, 0, gp, :], rhs=sbb[:, 0, gl, :], start=(j == 0), stop=False)
                                ins = e.matmul(pb[:, 0:128], lhsT=ctb[:, 1, gp, :], rhs=sbb[:, 1, gl, :], start=False, stop=(j == 3))
                            return ins
                        P.op("tensor", mmy, reads=[b_sbb, b_ctb], writes=[pbb])
                        v_(lambda e, pb=pb, i=i, ts_=ts_: e.scalar_tensor_tensor(ypre[:, 4 + i, :], uT[:, i, ts_], dsk[:, i:i + 1], pb[:, 0:128], ALU.mult, ALU.add),
                           [pbb, b_uT, b_sp], [b_ypre])

                def ssm_tail(tb):
                    a_(lambda e: e.activation(ypre[:, 4:8, :], ypre[:, 4:8, :], AF.Gelu), [b_ypre], [b_ypre])
                    ygb = ft1[:, 512:768].bitcast(BF16).rearrange("p (c t) -> p c t", t=128)
                    v_(lambda e: e.tensor_copy(ygb, ypre[:, 4:8, :]), [b_ypre], [b_ft[1]])
                    for j in range(4):
                        pb, pbb = next_bank()

                        def mmg(e, pb=pb, j=j):
                            ins = None
                            for i in range(4):
                                ins = e.matmul(pb[:, 0:128], lhsT=wglu[:, i, j * 128:(j + 1) * 128], rhs=ygb[:, i, :], start=(i == 0), stop=(i == 3))
                            return ins
                        P.op("tensor", mmg, reads=[b_ft[1], b_wglu], writes=[pbb])
                        sg_ = ft0[:, 640:768]
                        a_(lambda e, pb=pb: e.activation(sg_, pb[:, 0:128], AF.Sigmoid), [pbb], [b_ft[0]])
                        v_(lambda e, j=j: e.tensor_tensor(ypre[:, j, :], ypre[:, 4 + j, :], sg_, ALU.mult), [b_ft[0], b_ypre], [b_ypre])
                    rms_feature_major(ypre, b_ypre, ft1, ft0[:, 0:128], b_ft, 4, 512, gssm, 8, tb * 128)


                ssm_A(0, 0)
                inproj_qkv()
                P.fence("vector", fence_fn, [b_hT], b_E)
                emit_scores(0)
                for n in range(NB):
                    emit_scores(n + 1)
                    if n > 0:
                        ssm_A(n, 0)
                    emit_pv(n, 0, 4, False)
                    ssm_B(n, 0)
                    ssm_A(n, 1)
                    emit_pv(n, 4, 8, True)
                    ssm_B(n, 1)
                    ssm_tail(n)
                    emit_gmlp(n)
                if T + 1 < NT:
                    P.op("vector", lambda e: e.tensor_copy(kT[:, :, :, 0, :], kT[:, :, :, NB, :]), reads=[b_kT], writes=[b_kT])
                    P.op("vector", lambda e: e.tensor_copy(vtok[:, 0, :, :], vtok[:, NB, :, :]), reads=[b_vtok], writes=[b_vtok])


                for cb in range(4):
                    wv, bw_ = wget(J["out"][cb])
                    for tb in range(NB):
                        pb, pbb = next_bank()

                        def mmo(e, pb=pb, wv=wv, tb=tb):
                            ins = None
                            for kc in range(KC):
                                ins = e.matmul(pb[:], lhsT=ynT[:, kc, tb * 128:(tb + 1) * 128], rhs=wv[:, kc, :], start=(kc == 0), stop=(kc == KC - 1))
                            return ins
                        P.op("tensor", mmo, reads=[b_ynT] + bw_, writes=[pbb])
                        v_(lambda e, pb=pb, tb=tb, cb=cb: e.tensor_tensor(xt[:, tb, cb * 512:(cb + 1) * 512], xt[:, tb, cb * 512:(cb + 1) * 512], pb[:], ALU.add),
                           [pbb, b_xt[tb]], [b_xt[tb]])

                P.fence("vector", fence_fn, b_E, [b_hT])
                for tb in range(NB):
                    norm_block_to_hT(xt[:, tb, :], b_xt[tb], gffn, b_misc, hT, b_hT, tb * 128)
                P.fence("vector", fence_fn, mixer_bufs, b_act)
                for s_ in range(NST):
                    (jg, ju) = J["gu"][s_]
                    wgv, bwg = wget(jg)
                    wuv, bwu = wget(ju)
                    for hl in range(HPS):
                        hb = s_ * HPS + hl
                        pg, pgb = next_bank()
                        pu, pub = next_bank()

                        def mmgu(e, pg=pg, pu=pu, wgv=wgv, wuv=wuv, hl=hl):
                            ins = None
                            for kc in range(KC):
                                e.matmul(pg[:, 0:TT], lhsT=wgv[:, kc, hl * 128:(hl + 1) * 128], rhs=hT[:, kc, :], start=(kc == 0), stop=(kc == KC - 1))
                            for kc in range(KC):
                                ins = e.matmul(pu[:, 0:TT], lhsT=wuv[:, kc, hl * 128:(hl + 1) * 128], rhs=hT[:, kc, :], start=(kc == 0), stop=(kc == KC - 1))
                            return ins
                        P.op("tensor", mmgu, reads=[b_hT] + bwg + bwu, writes=[pgb, pub])
                        sgv, bsg = sgt[hb % 2][:, 0:TT], b_sgt[hb % 2]
                        a_(lambda e, pg=pg, sgv=sgv: e.activation(sgv, pg[:, 0:TT], AF.Silu), [pgb], [bsg])
                        v_(lambda e, pu=pu, sgv=sgv, hb=hb: e.tensor_tensor(actT[:, hb, :], sgv, pu[:, 0:TT], ALU.mult), [pub, bsg], [b_act[hb]])
                for cb in range(4):
                    accs = [next_bank() for _ in range(NB)]
                    for pi_, (a, b) in enumerate(pieces):
                        wv, bw_ = wget(J["dn"][cb][pi_])
                        for tb in range(NB):
                            def mmd(e, acc=accs[tb][0], wv=wv, tb=tb, a=a, b=b):
                                ins = None
                                for hb in range(a, b):
                                    ins = e.matmul(acc[:], lhsT=actT[:, hb, tb * 128:(tb + 1) * 128], rhs=wv[:, hb - a, :],
                                                   start=(hb == 0), stop=(hb == NHB - 1))
                                return ins
                            P.op("tensor", mmd, reads=b_act[a:b] + bw_, writes=[accs[tb][1]])
                    for tb in range(NB):
                        v_(lambda e, acc=accs[tb][0], tb=tb, cb=cb: e.tensor_tensor(xt[:, tb, cb * 512:(cb + 1) * 512], xt[:, tb, cb * 512:(cb + 1) * 512], acc[:], ALU.add),
                           [accs[tb][1], b_xt[tb]], [b_xt[tb]])
                for tb in range(NB):
                    if final_norm:
                        gfb = qTflat.bitcast(F32)[:, 0:D]
                        if tb == 0:
                            P.dma("sync", gfb, GFIN.partition_broadcast(128), reads=[], writes=[b_qT])
                        a_(lambda e, tb=tb: e.activation(xn[:], xt[:, tb, :], AF.Square, accum_out=nrm[:, 4:5]), [b_xt[tb]], [b_xn, b_nrm])
                        a_(lambda e: e.activation(nrm[:, 5:6], nrm[:, 4:5], AF.Sqrt, bias=eps_c, scale=1.0 / D), [b_nrm, b_cn], [b_nrm])
                        v_(lambda e: e.reciprocal(nrm[:, 6:7], nrm[:, 5:6]), [b_nrm], [b_nrm])
                        v_(lambda e, tb=tb: e.scalar_tensor_tensor(xt[:, tb, :], xt[:, tb, :], nrm[:, 6:7], gfb, ALU.mult, ALU.mult),
                           [b_xt[tb], b_nrm, b_qT], [b_xt[tb]])
                    P.dma("sync", Xdst[tok_base + tb * 128:tok_base + (tb + 1) * 128, :], xt[:, tb, :], reads=[b_xt[tb]],
                          writes=b_xdst, is_output=final_norm)


            if not final_norm:
                P.dma("sync", XHB, X1[NTOK - 128:NTOK, :], reads=[MB("x1d")], writes=[MB("xhb")])
                P.coll(lambda e: e.collective_compute("AllGather", ALU.bypass, replica_groups=[list(range(NCORES))],
                                                      ins=[XHB.opt()], outs=[XHG.opt()]),
                       reads=[MB("xhb")], writes=[MB("xhg")])

        emit_casts(0)
        for l in range(depth):
            emit_layer(l)
        P.finish()
        with nc.Block() as block:
            P.emit(block)
    return nc


def _ch_major(a):
    return np.ascontiguousarray(a.reshape(16, 2, 64).transpose(1, 2, 0).reshape(128, 16))


def _layer_small(inp, l):
    f = np.float32
    sp = np.zeros((128, 64), f)
    sp[:, 0:16] = _ch_major(inp["ssm_lam_re"][l])
    sp[:, 16:32] = _ch_major(inp["ssm_lam_im"][l])
    sp[:, 32:48] = _ch_major(np.repeat(inp["ssm_log_dt"][l][:, None], 64, axis=1))
    sp[:, 48:52] = inp["ssm_d"][l].reshape(4, 128).T
    sp[:, 52:56] = inp["out_norm_ssm"][l].reshape(4, 128).T
    sp[:, 56:60] = inp["out_norm_gmlp"][l].reshape(4, 128).T
    b32 = np.zeros((128, 2, 16, 32), f)
    ctp = np.zeros((128, 2, 16, 128), f)
    for ri, (bk, ck) in enumerate((("ssm_b_re", "ssm_c_re"), ("ssm_b_im", "ssm_c_im"))):
        Bm = inp[bk][l].reshape(16, 2, 64, 16)
        Cm = inp[ck][l].reshape(16, 2, 16, 64)
        for g2 in range(2):
            b32[g2 * 64:(g2 + 1) * 64, ri, :, g2 * 16:(g2 + 1) * 16] = Bm[:, g2].transpose(1, 0, 2)
            for gp in range(16):
                col = ((gp % 4) * 2 + g2) * 16
                ctp[g2 * 64:(g2 + 1) * 64, ri, gp, col:col + 16] = Cm[gp, g2].T
    return dict(sp=sp, b32=b32, ctp=ctp,
                gmix=np.ascontiguousarray(inp["norm_mix"][l].reshape(KC, 128).T))


def kernel(**inp):
    inp = {k: np.asarray(v) for k, v in inp.items()}
    x = inp["x"]
    Bsz, L, _ = x.shape
    depth = inp["w_in"].shape[0]
    DFF = inp["w_gate"].shape[2]
    ncores = NCORES
    cps = ncores // Bsz
    NTOK = L // cps
    xs = np.ascontiguousarray(x.reshape(ncores, NTOK, D))
    prog = build_fused(NTOK, DFF, depth)
    shared = dict(gfin=np.ascontiguousarray(inp["norm_final"][None, :]))
    for l in range(depth):
        sm = _layer_small(inp, l)
        sfx = "_%d" % l
        lay = dict(
            sp=sm["sp"], b32=sm["b32"], gmix=sm["gmix"], ctp=sm["ctp"],
            w_in=inp["w_in"][l], w_out=inp["w_out"][l], w_gate=inp["w_gate"][l], w_up=inp["w_up"][l], w_down=inp["w_down"][l],
            w_glu=inp["ssm_w_glu"][l],
            ws=np.ascontiguousarray(inp["gmlp_w_s"][l].transpose(1, 0, 2)),
            bs=np.ascontiguousarray(inp["gmlp_b_s"][l].reshape(1, 512)),
            lngb=np.ascontiguousarray(np.concatenate([inp["gmlp_ln_g"][l], inp["gmlp_ln_b"][l]])[None, :]),
            sinks=np.ascontiguousarray(inp["attn_sinks"][l][None, :]),
            gatt=np.ascontiguousarray(inp["out_norm_attn"][l].reshape(8, 128).T),
            gffn=np.ascontiguousarray(inp["norm_ffn"][l].reshape(KC, 128).T),
        )
        for k, v in lay.items():
            shared[k + sfx] = v
    in_maps = []
    for c in range(ncores):
        if c % cps == 0:
            xh = np.zeros((128, D), np.float32)
        else:
            xh = np.ascontiguousarray(xs[c - 1][NTOK - 128:NTOK])
        m = dict(x=xs[c], xh=xh, consts=host_consts(c, ncores, cps))
        m.update(shared)
        in_maps.append(m)
    res = run_bass_kernel_spmd(prog, in_maps, core_ids=list(range(ncores))).results
    out = np.stack([r["xo"] for r in res], axis=0)
    return np.ascontiguousarray(out.reshape(Bsz, L, D)).astype(np.float32, copy=False)
```
